# Optimizing a Trainium2 kernel written in Bass

```python
import math
import jax, jax.numpy as jnp
from jax import lax
import numpy as np

D_MODEL = 2048
BATCH = 2
SEQ = 8192
DEPTH = 4

HEAD_DIM = 64
D_MIX = D_MODEL
SWA_HEADS = D_MIX // (2 * HEAD_DIM)
SWA_KV_HEADS = SWA_HEADS // 4
SB_HEADS = D_MIX // (2 * HEAD_DIM)
GQA_GROUP = SWA_HEADS // SWA_KV_HEADS
WINDOW = 128
BLOCK = 128
N_BUCKETS = 32
MAX_DISTANCE = 128
D_FF = 4 * D_MODEL
EPS = 1e-6

SWA_Q_W = SWA_HEADS * HEAD_DIM
SWA_KV_W = SWA_KV_HEADS * HEAD_DIM
SB_W = SB_HEADS * HEAD_DIM
D_IN = SWA_Q_W + 2 * SWA_KV_W + 3 * SB_W
D_CAT = SWA_Q_W + SB_W

kernel_name = "hymba_swa_sink_stickbreaking_sqrelu"


def rmsnorm(x, g):
    xf = x.astype(jnp.float32)
    y = xf * lax.rsqrt(jnp.mean(xf * xf, axis=-1, keepdims=True) + EPS) * g.astype(jnp.float32)
    return y.astype(x.dtype)


def t5_causal_bucket_np(dist):
    max_exact = N_BUCKETS // 2
    d = np.maximum(dist, 0)
    ratio = np.maximum(d, 1).astype(np.float32) / max_exact
    large = max_exact + (np.log(ratio) / math.log(MAX_DISTANCE / max_exact)
                         * (N_BUCKETS - max_exact)).astype(np.int32)
    large = np.minimum(large, N_BUCKETS - 1)
    return np.where(d < max_exact, d, large).astype(np.int32)


def swa_attention(q, k, v, sinks, rel_bias):
    B, S = q.shape[0], q.shape[1]
    nb = S // BLOCK
    qb = q.reshape(B, nb, BLOCK, SWA_KV_HEADS, GQA_GROUP, HEAD_DIM)
    kb = k.reshape(B, nb, BLOCK, SWA_KV_HEADS, HEAD_DIM)
    vb = v.reshape(B, nb, BLOCK, SWA_KV_HEADS, HEAD_DIM)
    pad = jnp.zeros_like(kb[:, :1])
    k_prev = jnp.concatenate([pad, kb[:, :-1]], axis=1)
    v_prev = jnp.concatenate([pad, vb[:, :-1]], axis=1)
    k_ext = jnp.concatenate([k_prev, kb], axis=2)
    v_ext = jnp.concatenate([v_prev, vb], axis=2)

    qi = np.arange(BLOCK)[:, None]
    kj = np.arange(2 * BLOCK)[None, :]
    dist = qi + BLOCK - kj
    in_window = (dist >= 0) & (dist < WINDOW)
    not_pad = (np.arange(nb)[:, None, None] > 0) | (kj >= BLOCK)[None]
    valid = jnp.asarray(in_window[None] & not_pad).reshape(1, nb, 1, 1, BLOCK, 2 * BLOCK)

    bucket = jnp.asarray(t5_causal_bucket_np(dist))
    bias = jnp.take(rel_bias.astype(jnp.float32), bucket, axis=0)
    bias = bias.reshape(BLOCK, 2 * BLOCK, SWA_KV_HEADS, GQA_GROUP).transpose(2, 3, 0, 1)
    bias = bias.reshape(1, 1, SWA_KV_HEADS, GQA_GROUP, BLOCK, 2 * BLOCK)

    scores = jnp.einsum('bnqhgd,bnshd->bnhgqs', qb, k_ext).astype(jnp.float32)
    scores = scores * (1.0 / math.sqrt(HEAD_DIM)) + bias
    scores = jnp.where(valid, scores, -jnp.inf)
    sink = jnp.broadcast_to(sinks.astype(jnp.float32).reshape(1, 1, SWA_KV_HEADS, GQA_GROUP, 1, 1),
                            (B, nb, SWA_KV_HEADS, GQA_GROUP, BLOCK, 1))
    probs = jax.nn.softmax(jnp.concatenate([scores, sink], axis=-1), axis=-1)[..., :2 * BLOCK]
    out = jnp.einsum('bnhgqs,bnshd->bnqhgd', probs.astype(v.dtype), v_ext)
    return out.reshape(B, S, SWA_HEADS * HEAD_DIM)


def stick_breaking_attention(q, k, v):
    B, S = q.shape[0], q.shape[1]
    nb = S // BLOCK
    scale = 1.0 / math.sqrt(HEAD_DIM)
    qb = q.reshape(B, nb, BLOCK, SB_HEADS, HEAD_DIM).transpose(1, 0, 3, 2, 4)
    kt = k.transpose(0, 2, 1, 3)
    vt = v.transpose(0, 2, 1, 3)
    pos_k = jnp.arange(S, dtype=jnp.int32)
    blk_ids = jnp.arange(nb, dtype=jnp.int32)

    def block(args):
        q_blk, i = args
        pos_q = i * BLOCK + jnp.arange(BLOCK, dtype=jnp.int32)
        causal = (pos_k[None, :] < pos_q[:, None])[None, None]
        z = jnp.einsum('bhqd,bhsd->bhqs', q_blk, kt).astype(jnp.float32) * scale
        log_beta = jax.nn.log_sigmoid(z)
        log_1m = jnp.where(causal, jax.nn.log_sigmoid(-z), 0.0)
        later = lax.cumsum(log_1m, axis=3, reverse=True) - log_1m
        w = jnp.where(causal, jnp.exp(log_beta + later), 0.0)
        return jnp.einsum('bhqs,bhsd->bhqd', w.astype(v.dtype), vt)

    out = lax.map(block, (qb, blk_ids))
    return out.transpose(1, 0, 3, 2, 4).reshape(B, S, SB_HEADS * HEAD_DIM)


def setup_inputs(seed: int = 0) -> dict:
    key = jax.random.key(seed)
    ks = jax.random.split(key, 13)
    f32 = jnp.float32
    x = jax.random.normal(ks[0], (BATCH, SEQ, D_MODEL), f32)
    norm_attn_g = 1.0 + 0.02 * jax.random.normal(ks[1], (DEPTH, D_MODEL), f32)
    w_in = jax.random.normal(ks[2], (DEPTH, D_MODEL, D_IN), f32) * D_MODEL ** -0.5
    q_norm_g = 1.0 + 0.02 * jax.random.normal(ks[3], (DEPTH, HEAD_DIM), f32)
    k_norm_g = 1.0 + 0.02 * jax.random.normal(ks[4], (DEPTH, HEAD_DIM), f32)
    sinks = 0.5 * jax.random.normal(ks[5], (DEPTH, SWA_HEADS), f32)
    rel_bias = 0.5 * jax.random.normal(ks[6], (N_BUCKETS, SWA_HEADS), f32)
    swa_out_g = 1.0 + 0.02 * jax.random.normal(ks[7], (DEPTH, SWA_Q_W), f32)
    sb_out_g = 1.0 + 0.02 * jax.random.normal(ks[8], (DEPTH, SB_W), f32)
    w_out = jax.random.normal(ks[9], (DEPTH, D_CAT, D_MODEL), f32) * D_CAT ** -0.5
    norm_mlp_g = 1.0 + 0.02 * jax.random.normal(ks[10], (DEPTH, D_MODEL), f32)
    w_up = jax.random.normal(ks[11], (DEPTH, D_MODEL, D_FF), f32) * D_MODEL ** -0.5
    w_down = jax.random.normal(ks[12], (DEPTH, D_FF, D_MODEL), f32) * D_FF ** -0.5
    return {"x": x, "norm_attn_g": norm_attn_g, "w_in": w_in, "q_norm_g": q_norm_g,
            "k_norm_g": k_norm_g, "sinks": sinks, "rel_bias": rel_bias,
            "swa_out_g": swa_out_g, "sb_out_g": sb_out_g, "w_out": w_out,
            "norm_mlp_g": norm_mlp_g, "w_up": w_up, "w_down": w_down}


def reference(x, norm_attn_g, w_in, q_norm_g, k_norm_g, sinks, rel_bias,
              swa_out_g, sb_out_g, w_out, norm_mlp_g, w_up, w_down):
    B, S = x.shape[0], x.shape[1]
    o1 = SWA_Q_W
    o2 = o1 + SWA_KV_W
    o3 = o2 + SWA_KV_W
    o4 = o3 + SB_W
    o5 = o4 + SB_W
    for l in range(DEPTH):
        h = rmsnorm(x, norm_attn_g[l])
        proj = jnp.einsum('bsd,de->bse', h, w_in[l])
        q_a = proj[..., :o1].reshape(B, S, SWA_HEADS, HEAD_DIM)
        k_a = proj[..., o1:o2].reshape(B, S, SWA_KV_HEADS, HEAD_DIM)
        v_a = proj[..., o2:o3].reshape(B, S, SWA_KV_HEADS, HEAD_DIM)
        q_b = proj[..., o3:o4].reshape(B, S, SB_HEADS, HEAD_DIM)
        k_b = proj[..., o4:o5].reshape(B, S, SB_HEADS, HEAD_DIM)
        v_b = proj[..., o5:].reshape(B, S, SB_HEADS, HEAD_DIM)
        q_a = rmsnorm(q_a, q_norm_g[l])
        k_a = rmsnorm(k_a, k_norm_g[l])
        o_a = swa_attention(q_a, k_a, v_a, sinks[l], rel_bias)
        o_b = stick_breaking_attention(q_b, k_b, v_b)
        mix = jnp.concatenate([rmsnorm(o_a, swa_out_g[l]), rmsnorm(o_b, sb_out_g[l])], axis=-1)
        x = x + jnp.einsum('bse,ed->bsd', mix, w_out[l])
        h = rmsnorm(x, norm_mlp_g[l])
        u = jnp.einsum('bsd,df->bsf', h, w_up[l])
        x = x + jnp.einsum('bsf,fd->bsd', jnp.square(jax.nn.relu(u)), w_down[l])
    return x
```

```python
import math
import numpy as np
import ml_dtypes
import concourse.bass as bass
import concourse.mybir as mybir
from concourse.bass_utils import run_bass_kernel_spmd

F32 = mybir.dt.float32
BF16 = mybir.dt.bfloat16
AF = mybir.ActivationFunctionType
ALU = mybir.AluOpType
AX = mybir.AxisListType

D = 2048
B = 2
S = 8192
DEPTH = 4
DH = 64
NQA = 16
NKA = 4
NSB = 16
D_IN = 4608
D_FF = 8192
EPS = 1e-6
NCORES = 8
TOK = (B * S) // NCORES
TT = 512
NEG = -30000.0


class Buf:
    __slots__ = ("name", "writer", "readers")

    def __init__(self, name):
        self.name = name
        self.writer = None
        self.readers = {}


class K:
    def __init__(self, nc, stack):
        self.nc = nc
        self.eng = {"pe": nc.tensor, "act": nc.scalar, "dve": nc.vector, "pool": nc.gpsimd, "sp": nc.sync}
        self.sem = {}
        self.cnt = {}
        for e in ("pe", "act", "dve", "pool"):
            self.sem[e] = stack.enter_context(nc.semaphore("s_" + e))
            self.cnt[e] = 0
        self.seen = {e: {} for e in self.eng}
        self.stack = stack
        self.dma_sems = {}
        self.dma_cnt = {}
        self.semobj = {e: self.sem[e] for e in self.sem}
        self.out_tokens = []

    def dma_sem(self, key):
        if key not in self.dma_sems:
            self.dma_sems[key] = self.stack.enter_context(self.nc.semaphore("d_" + key))
            self.dma_cnt[key] = 0
            self.semobj["d_" + key] = self.dma_sems[key]
        return self.dma_sems[key]

    def _wait(self, e, tok):
        if tok is None:
            return
        key, val = tok
        if key == e and e == "pe":
            return
        if self.seen[e].get(key, 0) >= val:
            return
        self.eng[e].wait_ge(self.semobj[key], val)
        self.seen[e][key] = val

    def _deps(self, e, reads, writes):
        for b in reads:
            self._wait(e, b.writer)
        for b in writes:
            self._wait(e, b.writer)
            for t in list(b.readers.items()):
                if t[0] != e:
                    self._wait(e, t)

    def _mark(self, tok, reads, writes):
        for b in reads:
            if b.readers.get(tok[0], 0) < tok[1]:
                b.readers[tok[0]] = tok[1]
        for b in writes:
            b.writer = tok
            b.readers = {}

    def op(self, e, fn, reads=(), writes=(), inc=True):
        self._deps(e, reads, writes)
        ins = fn()
        if inc:
            self.cnt[e] += 1
            ins.then_inc(self.sem[e], 1)
            tok = (e, self.cnt[e])
        else:
            tok = (e, self.cnt[e] + 1)
        self._mark(tok, reads, writes)
        return tok

    def dma(self, q, key, out, in_, reads=(), writes=()):
        sem = self.dma_sem(key)
        self._deps(q, reads, writes)
        self.eng[q].dma_start(out=out, in_=in_).then_inc(sem, 16)
        self.dma_cnt[key] += 16
        tok = ("d_" + key, self.dma_cnt[key])
        self._mark(tok, reads, writes)
        return tok

    def finish(self, e="sp"):
        last = {}
        for t in self.out_tokens:
            last[t[0]] = max(last.get(t[0], 0), t[1])
        for t in last.items():
            self._wait(e, t)
        for x in ("pe", "act", "dve", "pool"):
            if self.cnt[x]:
                self._wait(e, (x, self.cnt[x]))


def _bc(ap, shape):
    return ap.to_broadcast(list(shape))


def emit_norm_transpose(k, nc, src_dram_rows, hT, tb, g_bc, ident, xs, hb, ss, rs, tp, halves, junk=None,
                        keep=None, src_sb=None):
    if src_sb is not None:
        xt, xb = src_sb
    else:
        xt, xb = xs
        k.dma("sp", xb.name, xt[:], src_dram_rows, writes=[xb])
    if junk is None:
        junk = hb
    k.op("dve", lambda: nc.vector.memset(ss[0][:], 0.0), writes=[ss[1]])
    for i, (lo, hi) in enumerate(halves):
        k.op("act", lambda lo=lo, hi=hi, i=i: nc.scalar.activation(
            out=junk[0][:, 0:hi - lo], in_=xt[:, lo:hi], func=AF.Square, accum_out=ss[0][:, i:i + 1]),
            reads=[xb], writes=[junk[1], ss[1]])
    nh = len(halves)
    w = halves[0][1] - halves[0][0]
    k.op("dve", lambda: nc.vector.tensor_scalar(out=rs[0][:, 0:nh], in0=ss[0][:, 0:nh], scalar1=1.0 / w,
                                                scalar2=EPS, op0=ALU.mult, op1=ALU.add),
         reads=[ss[1]], writes=[rs[1]])
    k.op("act", lambda: nc.scalar.activation(out=rs[0][:, 0:nh], in_=rs[0][:, 0:nh], func=AF.Sqrt),
         reads=[rs[1]], writes=[rs[1]])
    k.op("dve", lambda: nc.vector.reciprocal(out=rs[0][:, 0:nh], in_=rs[0][:, 0:nh]),
         reads=[rs[1]], writes=[rs[1]])
    for i, (lo, hi) in enumerate(halves):
        k.op("dve", lambda lo=lo, hi=hi, i=i: nc.vector.scalar_tensor_tensor(
            out=hb[0][:, lo:hi], in0=xt[:, lo:hi], scalar=rs[0][:, i:i + 1], in1=g_bc[0][:, lo:hi],
            op0=ALU.mult, op1=ALU.mult), reads=[xb, rs[1], g_bc[1]], writes=[hb[1]])
    if keep is not None:
        k.op("pool", lambda: nc.gpsimd.tensor_copy(out=keep[0][:], in_=xt[:]), reads=[xb], writes=[keep[1]])
    for cg in range(4):
        tpt, tpb = tp[cg % len(tp)]
        for j in range(4):
            c = cg * 4 + j
            k.op("pe", lambda c=c, j=j, tpt=tpt: nc.tensor.transpose(
                out=tpt[:, j, :], in_=hb[0][:, c * 128:(c + 1) * 128], identity=ident[0][:]),
                reads=[hb[1], ident[1]], writes=[tpb], inc=(j == 3))
        eng = "act" if cg % 2 == 0 else "dve"
        if eng == "act":
            k.op("act", lambda cg=cg, tpt=tpt: nc.scalar.copy(
                out=hT[0][:, cg * 4:cg * 4 + 4, tb * 128:(tb + 1) * 128], in_=tpt[:]),
                reads=[tpb], writes=[hT[1]])
        else:
            k.op("dve", lambda cg=cg, tpt=tpt: nc.vector.tensor_copy(
                out=hT[0][:, cg * 4:cg * 4 + 4, tb * 128:(tb + 1) * 128], in_=tpt[:]),
                reads=[tpb], writes=[hT[1]])


def load_bc(k, nc, tile_buf, dram_row_ap, key):
    t, b = tile_buf
    k.dma("sp", key, t[:], dram_row_ap.partition_broadcast(128), writes=[b])


def build_L1(ntok=TOK):
    from contextlib import ExitStack
    nc = bass.Bass("TRN2", target_bir_lowering=False)
    x = nc.dram_tensor("x", [ntok, D], F32, kind="ExternalInput").ap()
    w_in = nc.dram_tensor("w_in", [D, D_IN], F32, kind="ExternalInput").ap()
    g_attn = nc.dram_tensor("g_attn", [1, D], F32, kind="ExternalInput").ap()
    g_qk = nc.dram_tensor("g_qk", [1, 128], F32, kind="ExternalInput").ap()
    ident_d = nc.dram_tensor("ident", [128, 128], F32, kind="ExternalInput").ap()
    proj = nc.dram_tensor("proj", [ntok, D_IN], BF16, kind="ExternalOutput").ap()
    wv = w_in.rearrange("(k p) n -> p k n", p=128)
    scale = 1.0 / math.sqrt(DH)
    with ExitStack() as st:
        k = K(nc, st)
        sb = lambda name, shape, dt: st.enter_context(nc.sbuf_tensor(name, shape, dt))
        ps = lambda name, shape, dt: st.enter_context(nc.psum_tensor(name, shape, dt))
        T = lambda name, shape, dt: (sb(name, shape, dt), Buf(name))
        P = lambda name, shape, dt: (ps(name, shape, dt), Buf(name))
        g_bc = T("g_bc", [128, D], F32)
        gq = T("gq", [128, 128], F32)
        ident32 = T("ident32", [128, 128], F32)
        ident = T("identb", [128, 128], BF16)
        xs = [T("xs%d" % i, [128, D], F32) for i in range(2)]
        hb = [T("hb%d" % i, [128, D], BF16) for i in range(2)]
        junk = T("junk", [128, D], F32)
        ss = [T("ss%d" % i, [128, 2], F32) for i in range(2)]
        rs = [T("rs%d" % i, [128, 2], F32) for i in range(2)]
        hT = [T("hT%d" % i, [128, 16, TT], BF16) for i in range(2)]
        wr = [T("w%d" % i, [128, 16, 512], BF16) for i in range(2)]
        sq = [T("sq%d" % i, [128, 512], F32) for i in range(2)]
        qs = [T("qs%d" % i, [128, 8], F32) for i in range(2)]
        qn = [T("qn%d" % i, [128, 512], F32) for i in range(2)]
        ob = [T("ob%d" % i, [128, 512], BF16) for i in range(3)]
        tp = [P("tp%d" % i, [128, 4, 128], BF16) for i in range(2)]
        acc = [P("acc%d" % i, [128, 512], F32) for i in range(4)]
        with nc.Block():
            load_bc(k, nc, g_bc, g_attn, "g_bc")
            load_bc(k, nc, gq, g_qk, "gq")
            k.dma("sp", "ident32", ident32[0][:], ident_d, writes=[ident32[1]])
            k.op("dve", lambda: nc.vector.tensor_copy(out=ident[0][:], in_=ident32[0][:]),
                 reads=[ident32[1]], writes=[ident[1]])
            nwl = 0
            nob = 0
            nacc = 0
            for tt in range(ntok // TT):
                hTt = hT[tt % 2]
                for tb in range(TT // 128):
                    i = (tt * 4 + tb) % 2
                    r0 = tt * TT + tb * 128
                    emit_norm_transpose(k, nc, x[r0:r0 + 128, :], hTt, tb, g_bc, ident, xs[i], hb[i],
                                        ss[i], rs[i], tp, [(0, D)], junk)
                for pc in range(9):
                    wt, wb = wr[nwl % 2]
                    nwl += 1
                    k.dma("pool", wb.name, wt[:], wv[:, :, pc * 512:(pc + 1) * 512], writes=[wb])
                    for tb in range(TT // 128):
                        at, ab = acc[nacc % 4]
                        nacc += 1
                        for kc in range(16):
                            k.op("pe", lambda kc=kc, at=at, wt=wt, tb=tb: nc.tensor.matmul(
                                at[:], lhsT=hTt[0][:, kc, tb * 128:(tb + 1) * 128], rhs=wt[:, kc, :],
                                start=(kc == 0), stop=(kc == 15)),
                                reads=[hTt[1], wb], writes=[ab], inc=(kc == 15))
                        ot, obb = ob[nob % 3]
                        nob += 1
                        r0 = tt * TT + tb * 128
                        if pc in (0, 1, 2):
                            j = (tt * 36 + pc * 4 + tb) % 2
                            nq = 512 if pc < 2 else 256
                            nh = nq // DH
                            goff = 0 if pc < 2 else 64
                            sc = scale if pc < 2 else 1.0
                            k.op("act", lambda at=at, j=j, nq=nq: nc.scalar.activation(
                                out=sq[j][0][:, 0:nq], in_=at[:, 0:nq], func=AF.Square),
                                reads=[ab], writes=[sq[j][1]])
                            k.op("dve", lambda j=j, nq=nq, nh=nh: nc.vector.tensor_reduce(
                                out=qs[j][0][:, 0:nh], in_=sq[j][0][:, 0:nq].rearrange("p (h d) -> p h d", d=DH),
                                axis=AX.X, op=ALU.add), reads=[sq[j][1]], writes=[qs[j][1]])
                            k.op("dve", lambda j=j, nh=nh: nc.vector.tensor_scalar(
                                out=qs[j][0][:, 0:nh], in0=qs[j][0][:, 0:nh], scalar1=1.0 / DH, scalar2=EPS,
                                op0=ALU.mult, op1=ALU.add), reads=[qs[j][1]], writes=[qs[j][1]])
                            k.op("act", lambda j=j, nh=nh: nc.scalar.activation(
                                out=qs[j][0][:, 0:nh], in_=qs[j][0][:, 0:nh], func=AF.Sqrt),
                                reads=[qs[j][1]], writes=[qs[j][1]])
                            k.op("dve", lambda j=j, nh=nh: nc.vector.reciprocal(
                                out=qs[j][0][:, 0:nh], in_=qs[j][0][:, 0:nh]),
                                reads=[qs[j][1]], writes=[qs[j][1]])
                            k.op("dve", lambda at=at, j=j, nq=nq, nh=nh: nc.vector.tensor_tensor(
                                out=qn[j][0][:, 0:nq].rearrange("p (h d) -> p h d", d=DH),
                                in0=at[:, 0:nq].rearrange("p (h d) -> p h d", d=DH),
                                in1=_bc(qs[j][0][:, 0:nh].unsqueeze(2), [128, nh, DH]), op=ALU.mult),
                                reads=[ab, qs[j][1]], writes=[qn[j][1]])
                            k.op("dve", lambda ot=ot, j=j, nq=nq, nh=nh, goff=goff, sc=sc:
                                 nc.vector.scalar_tensor_tensor(
                                out=ot[:, 0:nq].rearrange("p (h d) -> p h d", d=DH),
                                in0=qn[j][0][:, 0:nq].rearrange("p (h d) -> p h d", d=DH), scalar=sc,
                                in1=_bc(gq[0][:, goff:goff + DH].unsqueeze(1), [128, nh, DH]),
                                op0=ALU.mult, op1=ALU.mult),
                                reads=[qn[j][1], gq[1]], writes=[obb])
                            if nq < 512:
                                k.op("act", lambda ot=ot, at=at, nq=nq: nc.scalar.copy(
                                    out=ot[:, nq:512], in_=at[:, nq:512]), reads=[ab], writes=[obb])
                        elif pc in (3, 4):
                            k.op("act", lambda ot=ot, at=at: nc.scalar.mul(ot[:], at[:], scale),
                                 reads=[ab], writes=[obb])
                        else:
                            if tb % 2 == 0:
                                k.op("act", lambda ot=ot, at=at: nc.scalar.copy(out=ot[:], in_=at[:]),
                                     reads=[ab], writes=[obb])
                            else:
                                k.op("dve", lambda ot=ot, at=at: nc.vector.tensor_copy(out=ot[:], in_=at[:]),
                                     reads=[ab], writes=[obb])
                        tok = k.dma("sp", obb.name + "_st", proj[r0:r0 + 128, pc * 512:(pc + 1) * 512], ot[:],
                                    reads=[obb])
                        k.out_tokens.append(tok)
            k.finish("sp")
    return nc


def build_L3(ntok=TOK):
    from contextlib import ExitStack
    nc = bass.Bass("TRN2", target_bir_lowering=False)
    x = nc.dram_tensor("x", [ntok, D], F32, kind="ExternalInput").ap()
    o = nc.dram_tensor("o", [ntok, D], F32, kind="ExternalInput").ap()
    w_out = nc.dram_tensor("w_out", [D, D], F32, kind="ExternalInput").ap()
    w_up = nc.dram_tensor("w_up", [D, D_FF], F32, kind="ExternalInput").ap()
    w_down = nc.dram_tensor("w_down", [D_FF, D], F32, kind="ExternalInput").ap()
    g_cat = nc.dram_tensor("g_cat", [1, D], F32, kind="ExternalInput").ap()
    g_mlp = nc.dram_tensor("g_mlp", [1, D], F32, kind="ExternalInput").ap()
    ident_d = nc.dram_tensor("ident", [128, 128], F32, kind="ExternalInput").ap()
    xo = nc.dram_tensor("xo", [ntok, D], F32, kind="ExternalOutput").ap()
    wov = w_out.rearrange("(k p) n -> p k n", p=128)
    wuv = w_up.rearrange("(k p) n -> p k n", p=128)
    NTB = TT // 128
    with ExitStack() as st:
        k = K(nc, st)
        sb = lambda name, shape, dt: st.enter_context(nc.sbuf_tensor(name, shape, dt))
        ps = lambda name, shape, dt: st.enter_context(nc.psum_tensor(name, shape, dt))
        T = lambda name, shape, dt: (sb(name, shape, dt), Buf(name))
        P = lambda name, shape, dt: (ps(name, shape, dt), Buf(name))
        gc = T("gc", [128, D], F32)
        gm = T("gm", [128, D], F32)
        ident32 = T("ident32", [128, 128], F32)
        ident = T("identb", [128, 128], BF16)
        xs = T("xs", [128, D], F32)
        hb = T("hb", [128, D], BF16)
        ss = T("ss", [128, 2], F32)
        rs = T("rs", [128, 2], F32)
        mT = T("mT", [128, 16, TT], BF16)
        h2T = T("h2T", [128, 16, TT], BF16)
        x1 = [T("x1_%d" % i, [128, D], F32) for i in range(NTB)]
        aT = T("aT", [128, 64, TT], BF16)
        wr = [T("w%d" % i, [128, 16, 512], BF16) for i in range(2)]
        tmp = [T("tmp%d" % i, [128, 512], F32) for i in range(2)]
        ob = [T("ob%d" % i, [128, 512], F32) for i in range(3)]
        tp = [P("tp%d" % i, [128, 4, 128], BF16) for i in range(2)]
        acc = [P("acc%d" % i, [128, 512], F32) for i in range(4)]
        with nc.Block():
            load_bc(k, nc, gc, g_cat, "gc")
            load_bc(k, nc, gm, g_mlp, "gm")
            k.dma("sp", "ident32", ident32[0][:], ident_d, writes=[ident32[1]])
            k.op("dve", lambda: nc.vector.tensor_copy(out=ident[0][:], in_=ident32[0][:]),
                 reads=[ident32[1]], writes=[ident[1]])
            nwl = 0
            nob = 0
            nacc = 0
            ntmp = 0

            def wload(src):
                nonlocal nwl
                wt, wb = wr[nwl % 2]
                nwl += 1
                k.dma("pool", wb.name, wt[:], src, writes=[wb])
                return wt, wb

            for tt in range(ntok // TT):
                for tb in range(NTB):
                    r0 = tt * TT + tb * 128
                    k.dma("sp", x1[tb][1].name, x1[tb][0][:], x[r0:r0 + 128, :], writes=[x1[tb][1]])
                    emit_norm_transpose(k, nc, o[r0:r0 + 128, :], mT, tb, gc, ident, xs, hb, ss, rs, tp,
                                        [(0, 1024), (1024, 2048)])
                for n in range(4):
                    wt, wb = wload(wov[:, :, n * 512:(n + 1) * 512])
                    for tb in range(NTB):
                        at, ab = acc[nacc % 4]
                        nacc += 1
                        for kc in range(16):
                            k.op("pe", lambda kc=kc, at=at, wt=wt, tb=tb: nc.tensor.matmul(
                                at[:], lhsT=mT[0][:, kc, tb * 128:(tb + 1) * 128], rhs=wt[:, kc, :],
                                start=(kc == 0), stop=(kc == 15)),
                                reads=[mT[1], wb], writes=[ab], inc=(kc == 15))
                        k.op("dve", lambda at=at, tb=tb, n=n: nc.vector.tensor_tensor(
                            out=x1[tb][0][:, n * 512:(n + 1) * 512], in0=at[:],
                            in1=x1[tb][0][:, n * 512:(n + 1) * 512], op=ALU.add),
                            reads=[ab, x1[tb][1]], writes=[x1[tb][1]])
                for tb in range(NTB):
                    emit_norm_transpose(k, nc, None, h2T, tb, gm, ident, None, hb, ss, rs, tp, [(0, D)],
                                        src_sb=x1[tb])
                for fp in range(D_FF // 512):
                    wt, wb = wload(wuv[:, :, fp * 512:(fp + 1) * 512])
                    for fc in range(4):
                        at, ab = acc[nacc % 4]
                        nacc += 1
                        for kc in range(16):
                            k.op("pe", lambda kc=kc, at=at, wt=wt, fc=fc: nc.tensor.matmul(
                                at[:], lhsT=wt[:, kc, fc * 128:(fc + 1) * 128], rhs=h2T[0][:, kc, :],
                                start=(kc == 0), stop=(kc == 15)),
                                reads=[h2T[1], wb], writes=[ab], inc=(kc == 15))
                        tm, tmb = tmp[ntmp % 2]
                        ntmp += 1
                        k.op("act", lambda at=at, tm=tm: nc.scalar.activation(out=tm[:], in_=at[:], func=AF.Relu),
                             reads=[ab], writes=[tmb])
                        fidx = fp * 4 + fc
                        if fidx % 2 == 0:
                            k.op("dve", lambda tm=tm, fidx=fidx: nc.vector.tensor_tensor(
                                out=aT[0][:, fidx, :], in0=tm[:], in1=tm[:], op=ALU.mult),
                                reads=[tmb], writes=[aT[1]])
                        else:
                            k.op("pool", lambda tm=tm, fidx=fidx: nc.gpsimd.tensor_tensor(
                                out=aT[0][:, fidx, :], in0=tm[:], in1=tm[:], op=ALU.mult),
                                reads=[tmb], writes=[aT[1]])
                for n in range(4):
                    accs = [acc[(nacc + i) % 4] for i in range(NTB)]
                    nacc += NTB
                    for kp in range(4):
                        wt, wb = wload(w_down[kp * 2048:(kp + 1) * 2048, n * 512:(n + 1) * 512]
                                       .rearrange("(k p) n -> p k n", p=128))
                        for tb in range(NTB):
                            at, ab = accs[tb]
                            for kc in range(16):
                                last = (kp == 3 and kc == 15)
                                k.op("pe", lambda kc=kc, at=at, wt=wt, tb=tb, kp=kp, last=last: nc.tensor.matmul(
                                    at[:], lhsT=aT[0][:, kp * 16 + kc, tb * 128:(tb + 1) * 128], rhs=wt[:, kc, :],
                                    start=(kp == 0 and kc == 0), stop=last),
                                    reads=[aT[1], wb], writes=[ab], inc=(kc == 15))
                    for tb in range(NTB):
                        at, ab = accs[tb]
                        ot, obb = ob[nob % 3]
                        nob += 1
                        r0 = tt * TT + tb * 128
                        k.op("dve", lambda at=at, ot=ot, tb=tb, n=n: nc.vector.tensor_tensor(
                            out=ot[:], in0=at[:], in1=x1[tb][0][:, n * 512:(n + 1) * 512], op=ALU.add),
                            reads=[ab, x1[tb][1]], writes=[obb])
                        tok = k.dma("sp", obb.name + "_st", xo[r0:r0 + 128, n * 512:(n + 1) * 512], ot[:],
                                    reads=[obb])
                        k.out_tokens.append(tok)
            k.finish("sp")
    return nc


def build_L2(SL=S):
    from contextlib import ExitStack
    nc = bass.Bass("TRN2", target_bir_lowering=False)
    NKB = SL // 128
    NQT = SL // 512
    qbT_d = nc.dram_tensor("qbT", [128, 2, SL], BF16, kind="ExternalInput").ap()
    kbT_d = nc.dram_tensor("kbT", [128, 2, SL], BF16, kind="ExternalInput").ap()
    vb_d = nc.dram_tensor("vb", [128, NKB, 256], BF16, kind="ExternalInput").ap()
    qaT_d = nc.dram_tensor("qaT", [64, 4, SL], BF16, kind="ExternalInput").ap()
    kaT_d = nc.dram_tensor("kaT", [64, SL], BF16, kind="ExternalInput").ap()
    va_d = nc.dram_tensor("va", [128, NKB, 64], BF16, kind="ExternalInput").ap()
    biasT_d = nc.dram_tensor("biasT", [128, 2, 4, 128], F32, kind="ExternalInput").ap()
    maskT_d = nc.dram_tensor("maskT", [128, 2, 128], F32, kind="ExternalInput").ap()
    sink_d = nc.dram_tensor("sink_bc", [128, 4], F32, kind="ExternalInput").ap()
    masks_d = nc.dram_tensor("masks", [128, 4, 512], F32, kind="ExternalInput").ap()
    ntri_d = nc.dram_tensor("ntri", [128, 128], F32, kind="ExternalInput").ap()
    o_a = nc.dram_tensor("o_a", [SL, 256], F32, kind="ExternalOutput").ap()
    o_bT = nc.dram_tensor("o_bT", [256, SL], F32, kind="ExternalOutput").ap()
    with ExitStack() as st0:
        k = K(nc, st0)
        with nc.Block():
            with ExitStack() as st:
                sb = lambda name, shape, dt: st.enter_context(nc.sbuf_tensor(name, shape, dt))
                ps = lambda name, shape, dt: st.enter_context(nc.psum_tensor(name, shape, dt))
                T = lambda name, shape, dt: (sb(name, shape, dt), Buf(name))
                P = lambda name, shape, dt: (ps(name, shape, dt), Buf(name))
                qaT = T("qaT_s", [64, 4, SL], BF16)
                kaT = T("kaT_s", [64, SL], BF16)
                va1 = T("va1", [128, NKB, 65], BF16)
                biasT = T("biasT_s", [128, 2, 4, 128], F32)
                maskT = T("maskT_s", [128, 2, 128], F32)
                sinkt = T("sinkt", [128, 4], F32)
                esink = T("esink", [128, 4], F32)
                stt = [T("stt%d" % i, [128, 4, 128], F32) for i in range(2)]
                pT = [T("pT%d" % i, [128, 4, 128], BF16) for i in range(4)]
                den = [T("den%d" % i, [128, 4], F32) for i in range(2)]
                oa = [T("oa%d" % i, [128, 4, 64], F32) for i in range(2)]
                sps = [P("sps%d" % i, [128, 4, 128], F32) for i in range(2)]
                ops = [P("ops%d" % i, [128, 4, 65], F32) for i in range(2)]
                k.dma("sp", "qaT_s", qaT[0][:], qaT_d, writes=[qaT[1]])
                k.dma("sp", "kaT_s", kaT[0][:], kaT_d, writes=[kaT[1]])
                k.op("dve", lambda: nc.vector.memset(va1[0][:], 1.0), writes=[va1[1]])
                k.dma("sp", "va1", va1[0][:, :, 0:64], va_d, writes=[va1[1]])
                k.dma("sp", "biasT_s", biasT[0][:], biasT_d, writes=[biasT[1]])
                k.dma("sp", "maskT_s", maskT[0][:], maskT_d, writes=[maskT[1]])
                k.dma("sp", "sinkt", sinkt[0][:], sink_d, writes=[sinkt[1]])
                for j in range(2):
                    k.op("dve", lambda j=j: nc.vector.tensor_tensor(
                        out=biasT[0][:, j], in0=biasT[0][:, j],
                        in1=_bc(maskT[0][:, j].unsqueeze(1), [128, 4, 128]), op=ALU.add),
                        reads=[biasT[1], maskT[1]], writes=[biasT[1]])
                k.op("act", lambda: nc.scalar.activation(out=esink[0][:], in_=sinkt[0][:], func=AF.Exp),
                     reads=[sinkt[1]], writes=[esink[1]])
                nst = 0
                npt = 0
                for qb in range(NKB):
                    js = [1] if qb == 0 else [0, 1]
                    pts = []
                    for j in js:
                        kb = qb - 1 + j
                        spt, spb = sps[nst % 2]
                        stt_t, stt_b = stt[nst % 2]
                        nst += 1
                        k.op("pe", lambda spt=spt, kb=kb, qb=qb: nc.tensor.matmul(
                            spt[:], lhsT=kaT[0][:, kb * 128:(kb + 1) * 128],
                            rhs=qaT[0][:, :, qb * 128:(qb + 1) * 128], start=True, stop=True),
                            reads=[kaT[1], qaT[1]], writes=[spb])
                        k.op("dve", lambda spt=spt, stt_t=stt_t, j=j: nc.vector.tensor_tensor(
                            out=stt_t[:], in0=spt[:], in1=biasT[0][:, j], op=ALU.add),
                            reads=[spb, biasT[1]], writes=[stt_b])
                        ptt, ptb = pT[npt % 4]
                        npt += 1
                        k.op("act", lambda ptt=ptt, stt_t=stt_t: nc.scalar.activation(
                            out=ptt[:], in_=stt_t[:], func=AF.Exp), reads=[stt_b], writes=[ptb])
                        pts.append((ptt, ptb, kb))
                    opt, opb = ops[qb % 2]
                    for h in range(4):
                        for idx, (ptt, ptb, kb) in enumerate(pts):
                            k.op("pe", lambda opt=opt, ptt=ptt, kb=kb, h=h, idx=idx: nc.tensor.matmul(
                                opt[:, h, :], lhsT=ptt[:, h, :], rhs=va1[0][:, kb, :],
                                start=(idx == 0), stop=(idx == len(pts) - 1)),
                                reads=[ptb, va1[1]], writes=[opb], inc=(h == 3 and idx == len(pts) - 1))
                    dn, dnb = den[qb % 2]
                    oat, oab = oa[qb % 2]
                    k.op("dve", lambda dn=dn, opt=opt: nc.vector.tensor_tensor(
                        out=dn[:], in0=opt[:, :, 64], in1=esink[0][:], op=ALU.add),
                        reads=[opb, esink[1]], writes=[dnb])
                    k.op("dve", lambda dn=dn: nc.vector.reciprocal(out=dn[:], in_=dn[:]),
                         reads=[dnb], writes=[dnb])
                    k.op("dve", lambda dn=dn, opt=opt, oat=oat: nc.vector.tensor_tensor(
                        out=oat[:], in0=opt[:, :, 0:64], in1=_bc(dn[:].unsqueeze(2), [128, 4, 64]), op=ALU.mult),
                        reads=[opb, dnb], writes=[oab])
                    tok = k.dma("sp", oab.name + "_st", o_a[qb * 128:(qb + 1) * 128, :],
                                oat[:].rearrange("p h d -> p (h d)"), reads=[oab])
                    k.out_tokens.append(tok)
                for e in ("pe", "act", "dve", "pool"):
                    k.finish(e)
            with ExitStack() as st:
                sb = lambda name, shape, dt: st.enter_context(nc.sbuf_tensor(name, shape, dt))
                ps = lambda name, shape, dt: st.enter_context(nc.psum_tensor(name, shape, dt))
                T = lambda name, shape, dt: (sb(name, shape, dt), Buf(name))
                P = lambda name, shape, dt: (ps(name, shape, dt), Buf(name))
                qbT = T("qbT_s", [128, 2, SL], BF16)
                kbT = T("kbT_s", [128, 2, SL], BF16)
                vb = T("vb_s", [128, NKB, 256], BF16)
                m32 = T("m32", [128, 4, 512], F32)
                masks = T("masks_s", [128, 4, 512], BF16)
                t32 = T("t32", [128, 128], F32)
                ntri = T("ntri_s", [128, 128], BF16)
                nones = T("nones_s", [128, 128], BF16)
                e32 = [T("e32_%d" % i, [128, 512], F32) for i in range(2)]
                spr = [T("sp_%d" % i, [128, 512], BF16) for i in range(3)]
                wr_ = [T("wv_%d" % i, [128, 512], BF16) for i in range(2)]
                R32 = T("R32", [128, 512], F32)
                Rb = [T("Rb_%d" % i, [128, 512], BF16) for i in range(2)]
                ost = [T("ost_%d" % i, [64, 512], F32) for i in range(2)]
                zps = [P("zps%d" % i, [128, 512], F32) for i in range(2)]
                eps_ = [P("eps%d" % i, [128, 512], F32) for i in range(2)]
                oacc = [P("oacc%d" % i, [64, 512], F32) for i in range(2)]
                k.dma("sp", "qbT_s", qbT[0][:], qbT_d, writes=[qbT[1]])
                k.dma("sp", "kbT_s", kbT[0][:], kbT_d, writes=[kbT[1]])
                k.dma("sp", "vb_s", vb[0][:], vb_d, writes=[vb[1]])
                k.dma("sp", "m32", m32[0][:], masks_d, writes=[m32[1]])
                k.dma("sp", "t32", t32[0][:], ntri_d, writes=[t32[1]])
                k.op("dve", lambda: nc.vector.tensor_copy(out=masks[0][:], in_=m32[0][:]),
                     reads=[m32[1]], writes=[masks[1]])
                k.op("dve", lambda: nc.vector.tensor_copy(out=ntri[0][:], in_=t32[0][:]),
                     reads=[t32[1]], writes=[ntri[1]])
                k.op("dve", lambda: nc.vector.memset(nones[0][:], -1.0), writes=[nones[1]])

                units = []
                for h in range(4):
                    for qt in range(NQT):
                        for kb in range(4 * qt + 3, -1, -1):
                            units.append((h, qt, kb))
                state = {}

                def stageA(u, n):
                    h, qt, kb = u
                    hp, pb = h // 2, 64 * (h % 2)
                    zt, zb = zps[n % 2]
                    et, eb = e32[n % 2]
                    st_, sbb = spr[n % 3]
                    k.op("pe", lambda: nc.tensor.matmul(
                        zt[:], lhsT=kbT[0][pb:pb + 64, hp, kb * 128:(kb + 1) * 128],
                        rhs=qbT[0][pb:pb + 64, hp, qt * 512:(qt + 1) * 512], start=True, stop=True),
                        reads=[kbT[1], qbT[1]], writes=[zb])
                    k.op("act", lambda: nc.scalar.activation(out=et[:], in_=zt[:], func=AF.Exp),
                         reads=[zb], writes=[eb])
                    k.op("act", lambda: nc.scalar.activation(out=st_[:], in_=et[:], func=AF.Ln, bias=1.0),
                         reads=[eb], writes=[sbb])
                    i = kb - 4 * qt
                    if i >= 0:
                        k.op("pool", lambda: nc.gpsimd.tensor_tensor(
                            out=st_[:], in0=st_[:], in1=masks[0][:, i, :], op=ALU.mult),
                            reads=[sbb, masks[1]], writes=[sbb])
                    state[n] = (st_, sbb)

                def stageB(u, n):
                    h, qt, kb = u
                    hp, pb = h // 2, 64 * (h % 2)
                    st_, sbb = state.pop(n)
                    first = (kb == 4 * qt + 3)
                    et, eb = eps_[n % 2]
                    wt, wb = wr_[n % 2]
                    ot, obuf = oacc[(h * NQT + qt) % 2]
                    rbt, rbb = Rb[n % 2]
                    k.op("pe", lambda: nc.tensor.matmul(
                        et[:], lhsT=kbT[0][pb:pb + 64, hp, kb * 128:(kb + 1) * 128],
                        rhs=qbT[0][pb:pb + 64, hp, qt * 512:(qt + 1) * 512], start=True, stop=False),
                        reads=[kbT[1], qbT[1]], writes=[eb], inc=False)
                    k.op("pe", lambda: nc.tensor.matmul(et[:], lhsT=ntri[0][:], rhs=st_[:], start=False, stop=first),
                         reads=[ntri[1], sbb], writes=[eb], inc=first)
                    if not first:
                        k.op("pe", lambda: nc.tensor.matmul(et[:], lhsT=nones[0][:], rhs=rbt[:], start=False, stop=True),
                             reads=[nones[1], rbb], writes=[eb])
                    k.op("act", lambda: nc.scalar.activation(out=wt[:], in_=et[:], func=AF.Exp),
                         reads=[eb], writes=[wb])
                    i = kb - 4 * qt
                    if i >= 0:
                        k.op("dve", lambda: nc.vector.tensor_tensor(
                            out=wt[:], in0=wt[:], in1=masks[0][:, i, :], op=ALU.mult),
                            reads=[wb, masks[1]], writes=[wb])
                    k.op("pe", lambda: nc.tensor.matmul(
                        ot[:], lhsT=vb[0][:, kb, h * 64:(h + 1) * 64], rhs=wt[:], start=first, stop=(kb == 0)),
                        reads=[vb[1], wb], writes=[obuf])
                    if kb > 0:
                        nrb, nrbb = Rb[(n + 1) % 2]
                        if first:
                            k.op("pool", lambda: nc.gpsimd.tensor_copy(out=R32[0][:], in_=st_[:]),
                                 reads=[sbb], writes=[R32[1]])
                        else:
                            k.op("pool", lambda: nc.gpsimd.tensor_tensor(
                                out=R32[0][:], in0=R32[0][:], in1=st_[:], op=ALU.add),
                                reads=[sbb, R32[1]], writes=[R32[1]])
                        k.op("pool", lambda: nc.gpsimd.tensor_copy(out=nrb[:], in_=R32[0][:]),
                             reads=[R32[1]], writes=[nrbb])
                    else:
                        o_t, o_b = ost[(h * NQT + qt) % 2]
                        k.op("dve", lambda: nc.vector.tensor_copy(out=o_t[:], in_=ot[:]),
                             reads=[obuf], writes=[o_b])
                        tok = k.dma("sp", o_b.name + "_st", o_bT[h * 64:(h + 1) * 64, qt * 512:(qt + 1) * 512],
                                    o_t[:], reads=[o_b])
                        k.out_tokens.append(tok)

                stageA(units[0], 0)
                for n, u in enumerate(units):
                    if n + 1 < len(units):
                        stageA(units[n + 1], n + 1)
                    stageB(u, n)
                k.finish("sp")
                for e in ("pe", "act", "dve", "pool"):
                    k.finish(e)
    return nc


def _t5_bucket(dist):
    max_exact = 16
    d = np.maximum(dist, 0)
    ratio = np.maximum(d, 1).astype(np.float32) / max_exact
    large = max_exact + (np.log(ratio) / math.log(128 / max_exact) * (32 - max_exact)).astype(np.int32)
    large = np.minimum(large, 31)
    return np.where(d < max_exact, d, large).astype(np.int32)


def l2_constants():
    s = np.arange(128)[:, None, None]
    j = np.arange(2)[None, :, None]
    q = np.arange(128)[None, None, :]
    dist = q + 128 - (j * 128 + s)
    valid = (dist >= 0) & (dist < 128)
    maskT = np.where(valid, 0.0, NEG).astype(np.float32)
    bucket = _t5_bucket(dist)
    sa = (np.arange(4)[None, :, None] * 128 + np.arange(128)[:, None, None])
    masks = (sa < np.arange(512)[None, None, :]).astype(np.float32)
    ntri = -(np.arange(128)[:, None] >= np.arange(128)[None, :]).astype(np.float32)
    return bucket, maskT, masks, ntri


_PROGS = {}


def _prog(name):
    if name not in _PROGS:
        _PROGS[name] = {"L1": build_L1, "L2": build_L2, "L3": build_L3}[name]()
    return _PROGS[name]


def _run(name, in_maps):
    res = run_bass_kernel_spmd(_prog(name), in_maps, core_ids=list(range(NCORES)))
    return res.results


def kernel(x, norm_attn_g, w_in, q_norm_g, k_norm_g, sinks, rel_bias, swa_out_g, sb_out_g, w_out,
           norm_mlp_g, w_up, w_down):
    f32 = np.float32
    c = lambda a: np.ascontiguousarray(a)
    xs = c(np.asarray(x, f32).reshape(B * S, D))
    ident = np.eye(128, dtype=f32)
    bucket, maskT, masks, ntri = l2_constants()
    rel_bias = np.asarray(rel_bias, f32)
    NKB = S // 128
    for l in range(DEPTH):
        gqk = c(np.concatenate([np.asarray(q_norm_g[l], f32), np.asarray(k_norm_g[l], f32)])[None, :])
        wl = c(np.asarray(w_in[l], f32))
        ga = c(np.asarray(norm_attn_g[l], f32)[None, :])
        r1 = _run("L1", [{"x": xs[i * TOK:(i + 1) * TOK], "w_in": wl, "g_attn": ga, "g_qk": gqk, "ident": ident}
                         for i in range(NCORES)])
        proj = np.concatenate([r1[i]["proj"] for i in range(NCORES)], 0)
        del r1
        ins2 = []
        for core in range(NCORES):
            b, gi = core // 4, core % 4
            pb = proj[b * S:(b + 1) * S]
            qb = pb[:, 1536 + gi * 256:1536 + (gi + 1) * 256]
            kb = pb[:, 2560 + gi * 256:2560 + (gi + 1) * 256]
            vb = pb[:, 3584 + gi * 256:3584 + (gi + 1) * 256]
            qa = pb[:, gi * 256:(gi + 1) * 256]
            ka = pb[:, 1024 + gi * 64:1024 + (gi + 1) * 64]
            va = pb[:, 1280 + gi * 64:1280 + (gi + 1) * 64]
            biasT = rel_bias[bucket][:, :, :, 4 * gi:4 * gi + 4].transpose(0, 1, 3, 2)
            ins2.append({
                "qbT": c(qb.reshape(S, 2, 128).transpose(2, 1, 0)),
                "kbT": c(kb.reshape(S, 2, 128).transpose(2, 1, 0)),
                "vb": c(vb.reshape(NKB, 128, 256).transpose(1, 0, 2)),
                "qaT": c(qa.reshape(S, 4, 64).transpose(2, 1, 0)),
                "kaT": c(ka.T),
                "va": c(va.reshape(NKB, 128, 64).transpose(1, 0, 2)),
                "biasT": c(biasT.astype(f32)), "maskT": maskT,
                "sink_bc": c(np.broadcast_to(np.asarray(sinks[l], f32)[None, 4 * gi:4 * gi + 4], (128, 4))),
                "masks": masks, "ntri": ntri})
        del proj
        r2 = _run("L2", ins2)
        del ins2
        o = np.empty((B * S, D), f32)
        for core in range(NCORES):
            b, gi = core // 4, core % 4
            o[b * S:(b + 1) * S, gi * 256:(gi + 1) * 256] = r2[core]["o_a"]
            o[b * S:(b + 1) * S, 1024 + gi * 256:1024 + (gi + 1) * 256] = r2[core]["o_bT"].T
        del r2
        gcat = c(np.concatenate([np.asarray(swa_out_g[l], f32), np.asarray(sb_out_g[l], f32)])[None, :])
        gm = c(np.asarray(norm_mlp_g[l], f32)[None, :])
        wo, wu, wd = (c(np.asarray(a[l], f32)) for a in (w_out, w_up, w_down))
        r3 = _run("L3", [{"x": xs[i * TOK:(i + 1) * TOK], "o": o[i * TOK:(i + 1) * TOK], "w_out": wo, "w_up": wu,
                          "w_down": wd, "g_cat": gcat, "g_mlp": gm, "ident": ident} for i in range(NCORES)])
        xs = np.concatenate([r3[i]["xo"] for i in range(NCORES)], 0)
        del r3, o
    return xs.reshape(B, S, D).astype(f32)
```

```python
import math
import numpy as np
import ml_dtypes
import concourse.bass as bass
import concourse.mybir as mybir
from concourse.bass_utils import run_bass_kernel_spmd

F32 = mybir.dt.float32
BF16 = mybir.dt.bfloat16
AF = mybir.ActivationFunctionType
ALU = mybir.AluOpType
AX = mybir.AxisListType

D = 2048
B = 2
S = 8192
DEPTH = 4
DH = 64
NQA = 16
NKA = 4
NSB = 16
D_IN = 4608
D_FF = 8192
EPS = 1e-6
NCORES = 8
TOK = (B * S) // NCORES
TT = 512
NEG = -30000.0


class Buf:
    __slots__ = ("name", "writer", "readers")

    def __init__(self, name):
        self.name = name
        self.writer = None
        self.readers = {}


class K:
    def __init__(self, nc, stack):
        self.nc = nc
        self.eng = {"pe": nc.tensor, "act": nc.scalar, "dve": nc.vector, "pool": nc.gpsimd, "sp": nc.sync}
        self.sem = {}
        self.cnt = {}
        for e in ("pe", "act", "dve", "pool"):
            self.sem[e] = stack.enter_context(nc.semaphore("s_" + e))
            self.cnt[e] = 0
        self.seen = {e: {} for e in self.eng}
        self.stack = stack
        self.dma_sems = {}
        self.dma_cnt = {}
        self.semobj = {e: self.sem[e] for e in self.sem}
        self.out_tokens = []

    def dma_sem(self, key):
        if key not in self.dma_sems:
            self.dma_sems[key] = self.stack.enter_context(self.nc.semaphore("d_" + key))
            self.dma_cnt[key] = 0
            self.semobj["d_" + key] = self.dma_sems[key]
        return self.dma_sems[key]

    def _wait(self, e, tok):
        if tok is None:
            return
        key, val = tok
        if key == e and e == "pe":
            return
        if self.seen[e].get(key, 0) >= val:
            return
        self.eng[e].wait_ge(self.semobj[key], val)
        self.seen[e][key] = val

    def _deps(self, e, reads, writes):
        for b in reads:
            self._wait(e, b.writer)
        for b in writes:
            self._wait(e, b.writer)
            for t in list(b.readers.items()):
                if t[0] != e:
                    self._wait(e, t)

    def _mark(self, tok, reads, writes):
        for b in reads:
            if b.readers.get(tok[0], 0) < tok[1]:
                b.readers[tok[0]] = tok[1]
        for b in writes:
            b.writer = tok
            b.readers = {}

    def op(self, e, fn, reads=(), writes=(), inc=True):
        self._deps(e, reads, writes)
        ins = fn()
        if inc:
            self.cnt[e] += 1
            ins.then_inc(self.sem[e], 1)
            tok = (e, self.cnt[e])
        else:
            tok = (e, self.cnt[e] + 1)
        self._mark(tok, reads, writes)
        return tok

    def dma(self, q, key, out, in_, reads=(), writes=()):
        sem = self.dma_sem(key)
        self._deps(q, reads, writes)
        self.eng[q].dma_start(out=out, in_=in_).then_inc(sem, 16)
        self.dma_cnt[key] += 16
        tok = ("d_" + key, self.dma_cnt[key])
        self._mark(tok, reads, writes)
        return tok

    def finish(self, e="sp"):
        last = {}
        for t in self.out_tokens:
            last[t[0]] = max(last.get(t[0], 0), t[1])
        for t in last.items():
            self._wait(e, t)
        for x in ("pe", "act", "dve", "pool"):
            if self.cnt[x]:
                self._wait(e, (x, self.cnt[x]))


def _bc(ap, shape):
    return ap.to_broadcast(list(shape))


def emit_norm_transpose(k, nc, src_dram_rows, hT, tb, g_bc, ident, xs, hb, ss, rs, tp, halves, junk=None,
                        keep=None, src_sb=None):
    if src_sb is not None:
        xt, xb = src_sb
    else:
        xt, xb = xs
        k.dma("sp", xb.name, xt[:], src_dram_rows, writes=[xb])
    if junk is None:
        junk = hb
    k.op("dve", lambda: nc.vector.memset(ss[0][:], 0.0), writes=[ss[1]])
    for i, (lo, hi) in enumerate(halves):
        k.op("act", lambda lo=lo, hi=hi, i=i: nc.scalar.activation(
            out=junk[0][:, 0:hi - lo], in_=xt[:, lo:hi], func=AF.Square, accum_out=ss[0][:, i:i + 1]),
            reads=[xb], writes=[junk[1], ss[1]])
    nh = len(halves)
    w = halves[0][1] - halves[0][0]
    k.op("dve", lambda: nc.vector.tensor_scalar(out=rs[0][:, 0:nh], in0=ss[0][:, 0:nh], scalar1=1.0 / w,
                                                scalar2=EPS, op0=ALU.mult, op1=ALU.add),
         reads=[ss[1]], writes=[rs[1]])
    k.op("act", lambda: nc.scalar.activation(out=rs[0][:, 0:nh], in_=rs[0][:, 0:nh], func=AF.Sqrt),
         reads=[rs[1]], writes=[rs[1]])
    k.op("dve", lambda: nc.vector.reciprocal(out=rs[0][:, 0:nh], in_=rs[0][:, 0:nh]),
         reads=[rs[1]], writes=[rs[1]])
    for i, (lo, hi) in enumerate(halves):
        k.op("dve", lambda lo=lo, hi=hi, i=i: nc.vector.scalar_tensor_tensor(
            out=hb[0][:, lo:hi], in0=xt[:, lo:hi], scalar=rs[0][:, i:i + 1], in1=g_bc[0][:, lo:hi],
            op0=ALU.mult, op1=ALU.mult), reads=[xb, rs[1], g_bc[1]], writes=[hb[1]])
    if keep is not None:
        k.op("pool", lambda: nc.gpsimd.tensor_copy(out=keep[0][:], in_=xt[:]), reads=[xb], writes=[keep[1]])
    for cg in range(4):
        tpt, tpb = tp[cg % len(tp)]
        for j in range(4):
            c = cg * 4 + j
            k.op("pe", lambda c=c, j=j, tpt=tpt: nc.tensor.transpose(
                out=tpt[:, j, :], in_=hb[0][:, c * 128:(c + 1) * 128], identity=ident[0][:]),
                reads=[hb[1], ident[1]], writes=[tpb], inc=(j == 3))
        eng = "act" if cg % 2 == 0 else "dve"
        if eng == "act":
            k.op("act", lambda cg=cg, tpt=tpt: nc.scalar.copy(
                out=hT[0][:, cg * 4:cg * 4 + 4, tb * 128:(tb + 1) * 128], in_=tpt[:]),
                reads=[tpb], writes=[hT[1]])
        else:
            k.op("dve", lambda cg=cg, tpt=tpt: nc.vector.tensor_copy(
                out=hT[0][:, cg * 4:cg * 4 + 4, tb * 128:(tb + 1) * 128], in_=tpt[:]),
                reads=[tpb], writes=[hT[1]])


def load_bc(k, nc, tile_buf, dram_row_ap, key):
    t, b = tile_buf
    k.dma("sp", key, t[:], dram_row_ap.partition_broadcast(128), writes=[b])


def _barrier(k):
    toks = [(e, k.cnt[e]) for e in ("pe", "act", "dve", "pool") if k.cnt[e]]
    toks += [("d_" + key, c) for key, c in k.dma_cnt.items() if c]
    for e in ("pe", "act", "dve", "pool", "sp"):
        for t in toks:
            k._wait(e, t)


class Scope:
    def __init__(self, nc, st, pfx):
        self.nc, self.st, self.pfx = nc, st, pfx

    def T(self, name, shape, dt):
        return (self.st.enter_context(self.nc.sbuf_tensor(self.pfx + name, shape, dt)), Buf(name))

    def P(self, name, shape, dt):
        return (self.st.enter_context(self.nc.psum_tensor(self.pfx + name, shape, dt)), Buf(name))


def emit_P1(k, nc, pfx, SL, x_src, w_in, g_attn_row, g_qk_row, ident, scr):
    from contextlib import ExitStack
    wv = w_in.rearrange("(k p) n -> p k n", p=128)
    scale = 1.0 / math.sqrt(DH)
    qaT_v = scr["qaT"].rearrange("h p s -> p h s")
    kaT_v = scr["kaT2"].rearrange("h p s -> p h s")
    with ExitStack() as st:
        sc = Scope(nc, st, pfx)
        T, P = sc.T, sc.P
        g_bc = T("g_bc", [128, D], F32)
        gq = T("gq", [128, 128], F32)
        xs = [T("xs%d" % i, [128, D], F32) for i in range(2)]
        hb = [T("hb%d" % i, [128, D], BF16) for i in range(2)]
        ss = [T("ss%d" % i, [128, 2], F32) for i in range(2)]
        rs = [T("rs%d" % i, [128, 2], F32) for i in range(2)]
        hT = [T("hT%d" % i, [128, 16, TT], BF16) for i in range(2)]
        wr = [T("w%d" % i, [128, 16, 512], BF16) for i in range(2)]
        sq = [T("sq%d" % i, [128, 512], F32) for i in range(2)]
        qs = [T("qs%d" % i, [128, 8], F32) for i in range(2)]
        qn = [T("qn%d" % i, [128, 512], F32) for i in range(2)]
        ob = [T("ob%d" % i, [128, 512], BF16) for i in range(3)]
        kk = [T("kk%d" % i, [128, 4, 2, 64], BF16) for i in range(2)]
        qst = [T("qst%d" % i, [128, 4, 128], BF16) for i in range(2)]
        tp = [P("tp%d" % i, [128, 4, 128], BF16) for i in range(2)]
        acc = [P("acc%d" % i, [128, 512], F32) for i in range(4)]
        load_bc(k, nc, g_bc, g_attn_row, "g_bc")
        load_bc(k, nc, gq, g_qk_row, "gq")
        cn = {"w": 0, "ob": 0, "acc": 0, "tp": 0, "qst": 0, "kk": 0}

        def nxt(ring, key):
            r = ring[cn[key] % len(ring)]
            cn[key] += 1
            return r

        def transposes_out(src_fn, src_buf, dst_ap):
            tpt, tpb = nxt(tp, "tp")
            for j in range(4):
                k.op("pe", lambda j=j: nc.tensor.transpose(out=tpt[:, j, :], in_=src_fn(j), identity=ident[0][:]),
                     reads=[src_buf, ident[1]], writes=[tpb], inc=(j == 3))
            qt_, qtb = nxt(qst, "qst")
            k.op("dve", lambda: nc.vector.tensor_copy(out=qt_[:], in_=tpt[:]), reads=[tpb], writes=[qtb])
            k.dma("sp", qtb.name + "_st", dst_ap, qt_[:], reads=[qtb])

        for tt in range(SL // TT):
            hTt = hT[tt % 2]
            t0 = tt * TT
            for tb in range(TT // 128):
                i = (tt * 4 + tb) % 2
                r0 = t0 + tb * 128
                emit_norm_transpose(k, nc, x_src[r0:r0 + 128, :], hTt, tb, g_bc, ident, xs[i], hb[i],
                                    ss[i], rs[i], tp, [(0, D)])
            for pc in range(9):
                wt, wb = nxt(wr, "w")
                k.dma("pool", wb.name, wt[:], wv[:, :, pc * 512:(pc + 1) * 512], writes=[wb])
                if pc in (3, 4, 5, 6):
                    dst = scr["qbT"] if pc < 5 else scr["kbT"]
                    for fc in range(4):
                        at, ab = nxt(acc, "acc")
                        for kc in range(16):
                            k.op("pe", lambda kc=kc, fc=fc: nc.tensor.matmul(
                                at[:], lhsT=wt[:, kc, fc * 128:(fc + 1) * 128], rhs=hTt[0][:, kc, :],
                                start=(kc == 0), stop=(kc == 15)),
                                reads=[hTt[1], wb], writes=[ab], inc=(kc == 15))
                        ot, obb = nxt(ob, "ob")
                        if pc < 5:
                            k.op("act", lambda: nc.scalar.mul(ot[:], at[:], scale), reads=[ab], writes=[obb])
                        elif fc % 2 == 0:
                            k.op("act", lambda: nc.scalar.copy(out=ot[:], in_=at[:]), reads=[ab], writes=[obb])
                        else:
                            k.op("dve", lambda: nc.vector.tensor_copy(out=ot[:], in_=at[:]), reads=[ab], writes=[obb])
                        hpi = ((pc - 3) % 2) * 4 + fc
                        k.dma("sp", obb.name + "_st", dst[hpi, :, t0:t0 + TT], ot[:], reads=[obb])
                    continue
                for tb in range(TT // 128):
                    at, ab = nxt(acc, "acc")
                    for kc in range(16):
                        k.op("pe", lambda kc=kc, tb=tb: nc.tensor.matmul(
                            at[:], lhsT=hTt[0][:, kc, tb * 128:(tb + 1) * 128], rhs=wt[:, kc, :],
                            start=(kc == 0), stop=(kc == 15)),
                            reads=[hTt[1], wb], writes=[ab], inc=(kc == 15))
                    ot, obb = nxt(ob, "ob")
                    r0 = t0 + tb * 128
                    if pc in (0, 1, 2):
                        j = (tt * 36 + pc * 4 + tb) % 2
                        nq = 512 if pc < 2 else 256
                        nh = nq // DH
                        goff = 0 if pc < 2 else 64
                        sc_ = scale if pc < 2 else 1.0
                        k.op("act", lambda: nc.scalar.activation(out=sq[j][0][:, 0:nq], in_=at[:, 0:nq], func=AF.Square),
                             reads=[ab], writes=[sq[j][1]])
                        k.op("dve", lambda: nc.vector.tensor_reduce(
                            out=qs[j][0][:, 0:nh], in_=sq[j][0][:, 0:nq].rearrange("p (h d) -> p h d", d=DH),
                            axis=AX.X, op=ALU.add), reads=[sq[j][1]], writes=[qs[j][1]])
                        k.op("dve", lambda: nc.vector.tensor_scalar(
                            out=qs[j][0][:, 0:nh], in0=qs[j][0][:, 0:nh], scalar1=1.0 / DH, scalar2=EPS,
                            op0=ALU.mult, op1=ALU.add), reads=[qs[j][1]], writes=[qs[j][1]])
                        k.op("act", lambda: nc.scalar.activation(out=qs[j][0][:, 0:nh], in_=qs[j][0][:, 0:nh], func=AF.Sqrt),
                             reads=[qs[j][1]], writes=[qs[j][1]])
                        k.op("dve", lambda: nc.vector.reciprocal(out=qs[j][0][:, 0:nh], in_=qs[j][0][:, 0:nh]),
                             reads=[qs[j][1]], writes=[qs[j][1]])
                        k.op("dve", lambda: nc.vector.tensor_tensor(
                            out=qn[j][0][:, 0:nq].rearrange("p (h d) -> p h d", d=DH),
                            in0=at[:, 0:nq].rearrange("p (h d) -> p h d", d=DH),
                            in1=_bc(qs[j][0][:, 0:nh].unsqueeze(2), [128, nh, DH]), op=ALU.mult),
                            reads=[ab, qs[j][1]], writes=[qn[j][1]])
                        k.op("dve", lambda: nc.vector.scalar_tensor_tensor(
                            out=ot[:, 0:nq].rearrange("p (h d) -> p h d", d=DH),
                            in0=qn[j][0][:, 0:nq].rearrange("p (h d) -> p h d", d=DH), scalar=sc_,
                            in1=_bc(gq[0][:, goff:goff + DH].unsqueeze(1), [128, nh, DH]),
                            op0=ALU.mult, op1=ALU.mult), reads=[qn[j][1], gq[1]], writes=[obb])
                        if pc < 2:
                            transposes_out(lambda jj: ot[:, jj * 128:(jj + 1) * 128], obb,
                                           qaT_v[:, pc * 4:pc * 4 + 4, r0:r0 + 128])
                        else:
                            k.op("act", lambda: nc.scalar.copy(out=ot[:, 256:512], in_=at[:, 256:512]),
                                 reads=[ab], writes=[obb])
                            k.dma("sp", obb.name + "_st", scr["va"][r0:r0 + 128, :], ot[:, 256:512], reads=[obb])
                            kt, kkb = nxt(kk, "kk")
                            k.op("dve", lambda: nc.vector.tensor_copy(
                                out=kt[:, :, 0, :], in_=ot[:, 0:256].rearrange("p (g d) -> p g d", d=DH)),
                                reads=[obb], writes=[kkb])
                            k.op("pool", lambda: nc.gpsimd.tensor_copy(
                                out=kt[:, :, 1, :], in_=ot[:, 0:256].rearrange("p (g d) -> p g d", d=DH)),
                                reads=[obb], writes=[kkb])
                            transposes_out(lambda jj: kt[:, jj].rearrange("p a d -> p (a d)"), kkb,
                                           kaT_v[:, 0:4, r0:r0 + 128])
                    else:
                        if tb % 2 == 0:
                            k.op("act", lambda: nc.scalar.copy(out=ot[:], in_=at[:]), reads=[ab], writes=[obb])
                        else:
                            k.op("dve", lambda: nc.vector.tensor_copy(out=ot[:], in_=at[:]), reads=[ab], writes=[obb])
                        k.dma("sp", obb.name + "_st", scr["vb"][r0:r0 + 128, (pc - 7) * 512:(pc - 6) * 512], ot[:],
                              reads=[obb])
        _barrier(k)


def emit_P2(k, nc, pfx, SL, gi, scr, biasT_d, maskT_d, sink_all, sink_c0, masks_d, ntri_d, o_s, parts="ab"):
    from contextlib import ExitStack
    NKB = SL // 128
    NQT = SL // 512
    def _swa():
        with ExitStack() as st:
            sc = Scope(nc, st, pfx + "a_")
            T, P = sc.T, sc.P
            qaT = T("qaT_s", [128, 2, SL], BF16)
            kaT = T("kaT_s", [128, SL], BF16)
            va1 = T("va1", [128, NKB, 72], BF16)
            biasT = T("biasT_s", [128, 2, 16, 128], F32)
            maskT = T("maskT_s", [128, 2, 128], F32)
            sinkt = T("sinkt", [128, sink_all.shape[1]], F32)
            esink = T("esink", [128, sink_all.shape[1]], F32)
            stt = [T("stt%d" % i, [128, 4, 128], F32) for i in range(2)]
            pT = [T("pT%d" % i, [128, 4, 128], BF16) for i in range(4)]
            den = [T("den%d" % i, [128, 4], F32) for i in range(2)]
            oa = [T("oa%d" % i, [128, 4, 64], F32) for i in range(2)]
            sps = [[P("sps%d_%d" % (i, par), [128, 2, 128], F32) for par in range(2)] for i in range(2)]
            ops = [P("ops%d" % i, [128, 4, 65], F32) for i in range(2)]
            k.dma("sp", "qaT_s", qaT[0][:], scr["qaT"][2 * gi:2 * gi + 2].rearrange("h p s -> p h s"), writes=[qaT[1]])
            k.dma("sp", "kaT_s", kaT[0][:], scr["kaT2"][gi], writes=[kaT[1]])
            k.op("dve", lambda: nc.vector.memset(va1[0][:], 1.0), writes=[va1[1]])
            for c0 in range(0, NKB, 8):
                k.dma("sp", "va1", va1[0][:, c0:c0 + 8, 0:64],
                      scr["va"][c0 * 128:(c0 + 8) * 128, gi * 64:(gi + 1) * 64]
                      .rearrange("(kb p) d -> p kb d", p=128), writes=[va1[1]])
            k.dma("sp", "biasT_s", biasT[0][:], biasT_d, writes=[biasT[1]])
            k.dma("sp", "maskT_s", maskT[0][:], maskT_d, writes=[maskT[1]])
            k.dma("sp", "sinkt", sinkt[0][:], sink_all, writes=[sinkt[1]])
            for j in range(2):
                k.op("dve", lambda: nc.vector.tensor_tensor(
                    out=biasT[0][:, j], in0=biasT[0][:, j],
                    in1=_bc(maskT[0][:, j].unsqueeze(1), [128, 16, 128]), op=ALU.add),
                    reads=[biasT[1], maskT[1]], writes=[biasT[1]])
            k.op("act", lambda: nc.scalar.activation(out=esink[0][:], in_=sinkt[0][:], func=AF.Exp),
                 reads=[sinkt[1]], writes=[esink[1]])
            nst = 0
            npt = 0
            for qb in range(NKB):
                js = [1] if qb == 0 else [0, 1]
                pts = []
                for j in js:
                    kb = qb - 1 + j
                    sp2 = sps[nst % 2]
                    stt_t, stt_b = stt[nst % 2]
                    nst += 1
                    for h in range(4):
                        hp, par = h // 2, h % 2
                        pb = 64 * par
                        spt, spb = sp2[par]
                        k.op("pe", lambda: nc.tensor.matmul(
                            spt[:, hp, :], lhsT=kaT[0][pb:pb + 64, kb * 128:(kb + 1) * 128],
                            rhs=qaT[0][pb:pb + 64, hp, qb * 128:(qb + 1) * 128], start=True, stop=True),
                            reads=[kaT[1], qaT[1]], writes=[spb], inc=(h >= 2))
                    for par in range(2):
                        spt, spb = sp2[par]
                        k.op("dve", lambda: nc.vector.tensor_tensor(
                            out=stt_t[:].rearrange("p (a b) q -> p a b q", b=2)[:, :, par, :], in0=spt[:],
                            in1=biasT[0][:, j, 4 * gi:4 * gi + 4, :].rearrange("p (a b) q -> p a b q", b=2)[:, :, par, :],
                            op=ALU.add), reads=[spb, biasT[1]], writes=[stt_b])
                    ptt, ptb = pT[npt % 4]
                    npt += 1
                    k.op("act", lambda: nc.scalar.activation(out=ptt[:], in_=stt_t[:], func=AF.Exp),
                         reads=[stt_b], writes=[ptb])
                    pts.append((ptt, ptb, kb))
                opt, opb = ops[qb % 2]
                for h in range(4):
                    for idx, (ptt, ptb, kb) in enumerate(pts):
                        k.op("pe", lambda: nc.tensor.matmul(
                            opt[:, h, :], lhsT=ptt[:, h, :], rhs=va1[0][:, kb, 0:65],
                            start=(idx == 0), stop=(idx == len(pts) - 1)),
                            reads=[ptb, va1[1]], writes=[opb], inc=(h == 3 and idx == len(pts) - 1))
                dn, dnb = den[qb % 2]
                oat, oab = oa[qb % 2]
                k.op("dve", lambda: nc.vector.tensor_tensor(out=dn[:], in0=opt[:, :, 64], in1=esink[0][:, sink_c0:sink_c0 + 4], op=ALU.add),
                     reads=[opb, esink[1]], writes=[dnb])
                k.op("dve", lambda: nc.vector.reciprocal(out=dn[:], in_=dn[:]), reads=[dnb], writes=[dnb])
                k.op("dve", lambda: nc.vector.tensor_tensor(
                    out=oat[:], in0=opt[:, :, 0:64], in1=_bc(dn[:].unsqueeze(2), [128, 4, 64]), op=ALU.mult),
                    reads=[opb, dnb], writes=[oab])
                k.dma("sp", oab.name + "_st", o_s[qb * 128:(qb + 1) * 128, gi * 256:(gi + 1) * 256],
                      oat[:].rearrange("p h d -> p (h d)"), reads=[oab])
            _barrier(k)

    def _sb():
        with ExitStack() as st:
            sc = Scope(nc, st, pfx + "b_")
            T, P = sc.T, sc.P
            qbT = T("qbT_s", [128, 2, SL], BF16)
            kbT = T("kbT_s", [128, 2, SL], BF16)
            vb = T("vb_s", [128, NKB, 256], BF16)
            m32 = T("m32", [128, 4, 512], F32)
            masks = T("masks_s", [128, 4, 512], BF16)
            t32 = T("t32", [128, 128], F32)
            ntri = T("ntri_s", [128, 128], BF16)
            nones = T("nones_s", [128, 128], BF16)
            e32 = [T("e32_%d" % i, [128, 512], F32) for i in range(2)]
            spr = [T("sp_%d" % i, [128, 512], BF16) for i in range(3)]
            wr_ = [T("wv_%d" % i, [128, 512], BF16) for i in range(2)]
            R32 = T("R32", [128, 512], F32)
            Rb = [T("Rb_%d" % i, [128, 512], BF16) for i in range(2)]
            ost = [T("ost_%d" % i, [128, 4, 64], F32) for i in range(2)]
            zps = [P("zps%d" % i, [128, 512], F32) for i in range(2)]
            eps_ = [P("eps%d" % i, [128, 512], F32) for i in range(2)]
            oacc = [P("oacc%d" % i, [128, 4, 64], F32) for i in range(2)]
            k.dma("sp", "qbT_s", qbT[0][:], scr["qbT"][2 * gi:2 * gi + 2].rearrange("h p s -> p h s"), writes=[qbT[1]])
            k.dma("sp", "kbT_s", kbT[0][:], scr["kbT"][2 * gi:2 * gi + 2].rearrange("h p s -> p h s"), writes=[kbT[1]])
            for c0 in range(0, NKB, 8):
                k.dma("sp", "vb_s", vb[0][:, c0:c0 + 8, :],
                      scr["vb"][c0 * 128:(c0 + 8) * 128, gi * 256:(gi + 1) * 256]
                      .rearrange("(kb p) f -> p kb f", p=128), writes=[vb[1]])
            k.dma("sp", "m32", m32[0][:], masks_d, writes=[m32[1]])
            k.dma("sp", "t32", t32[0][:], ntri_d, writes=[t32[1]])
            k.op("dve", lambda: nc.vector.tensor_copy(out=masks[0][:], in_=m32[0][:]), reads=[m32[1]], writes=[masks[1]])
            k.op("dve", lambda: nc.vector.tensor_copy(out=ntri[0][:], in_=t32[0][:]), reads=[t32[1]], writes=[ntri[1]])
            k.op("dve", lambda: nc.vector.memset(nones[0][:], -1.0), writes=[nones[1]])
            units = [(h, qt, kb) for h in range(4) for qt in range(NQT) for kb in range(4 * qt + 3, -1, -1)]
            state = {}

            def stageA(u, n):
                h, qt, kb = u
                hp, pb = h // 2, 64 * (h % 2)
                zt, zb = zps[n % 2]
                et, eb = e32[n % 2]
                st_, sbb = spr[n % 3]
                k.op("pe", lambda: nc.tensor.matmul(
                    zt[:], lhsT=kbT[0][pb:pb + 64, hp, kb * 128:(kb + 1) * 128],
                    rhs=qbT[0][pb:pb + 64, hp, qt * 512:(qt + 1) * 512], start=True, stop=True),
                    reads=[kbT[1], qbT[1]], writes=[zb])
                k.op("act", lambda: nc.scalar.activation(out=et[:], in_=zt[:], func=AF.Exp), reads=[zb], writes=[eb])
                k.op("act", lambda: nc.scalar.activation(out=st_[:], in_=et[:], func=AF.Ln, bias=1.0),
                     reads=[eb], writes=[sbb])
                i = kb - 4 * qt
                if i >= 0:
                    k.op("pool", lambda: nc.gpsimd.tensor_tensor(out=st_[:], in0=st_[:], in1=masks[0][:, i, :], op=ALU.mult),
                         reads=[sbb, masks[1]], writes=[sbb])
                state[n] = (st_, sbb)

            def stageB(u, n):
                h, qt, kb = u
                hp, pb = h // 2, 64 * (h % 2)
                st_, sbb = state.pop(n)
                first = (kb == 4 * qt + 3)
                et, eb = eps_[n % 2]
                wt, wb = wr_[n % 2]
                ot, obuf = oacc[(h * NQT + qt) % 2]
                rbt, rbb = Rb[n % 2]
                k.op("pe", lambda: nc.tensor.matmul(
                    et[:], lhsT=kbT[0][pb:pb + 64, hp, kb * 128:(kb + 1) * 128],
                    rhs=qbT[0][pb:pb + 64, hp, qt * 512:(qt + 1) * 512], start=True, stop=False),
                    reads=[kbT[1], qbT[1]], writes=[eb], inc=False)
                k.op("pe", lambda: nc.tensor.matmul(et[:], lhsT=ntri[0][:], rhs=st_[:], start=False, stop=first),
                     reads=[ntri[1], sbb], writes=[eb], inc=first)
                if not first:
                    k.op("pe", lambda: nc.tensor.matmul(et[:], lhsT=nones[0][:], rhs=rbt[:], start=False, stop=True),
                         reads=[nones[1], rbb], writes=[eb])
                k.op("act", lambda: nc.scalar.activation(out=wt[:], in_=et[:], func=AF.Exp), reads=[eb], writes=[wb])
                i = kb - 4 * qt
                if i >= 0:
                    k.op("dve", lambda: nc.vector.tensor_tensor(out=wt[:], in0=wt[:], in1=masks[0][:, i, :], op=ALU.mult),
                         reads=[wb, masks[1]], writes=[wb])
                for sub in range(4):
                    k.op("pe", lambda: nc.tensor.matmul(
                        ot[:, sub, :], lhsT=wt[:, sub * 128:(sub + 1) * 128], rhs=vb[0][:, kb, h * 64:(h + 1) * 64],
                        start=first, stop=(kb == 0)), reads=[vb[1], wb], writes=[obuf], inc=(sub == 3))
                if kb > 0:
                    nrb, nrbb = Rb[(n + 1) % 2]
                    if first:
                        k.op("pool", lambda: nc.gpsimd.tensor_copy(out=R32[0][:], in_=st_[:]), reads=[sbb], writes=[R32[1]])
                    else:
                        k.op("pool", lambda: nc.gpsimd.tensor_tensor(out=R32[0][:], in0=R32[0][:], in1=st_[:], op=ALU.add),
                             reads=[sbb, R32[1]], writes=[R32[1]])
                    k.op("pool", lambda: nc.gpsimd.tensor_copy(out=nrb[:], in_=R32[0][:]), reads=[R32[1]], writes=[nrbb])
                else:
                    o_t, o_b = ost[(h * NQT + qt) % 2]
                    k.op("dve", lambda: nc.vector.tensor_copy(out=o_t[:], in_=ot[:]), reads=[obuf], writes=[o_b])
                    c0 = 1024 + (4 * gi + h) * 64
                    k.dma("sp", o_b.name + "_st",
                          o_s[qt * 512:(qt + 1) * 512, c0:c0 + 64].rearrange("(sub p) d -> p sub d", p=128),
                          o_t[:], reads=[o_b])

            stageA(units[0], 0)
            for n, u in enumerate(units):
                if n + 1 < len(units):
                    stageA(units[n + 1], n + 1)
                stageB(u, n)
            _barrier(k)

    if "a" in parts:
        _swa()
    if "b" in parts:
        _sb()


def emit_P3(k, nc, pfx, SL, x_src, x_dst, o_s, w_out, w_up, w_down, g_cat_row, g_mlp_row, ident):
    from contextlib import ExitStack
    wov = w_out.rearrange("(k p) n -> p k n", p=128)
    wuv = w_up.rearrange("(k p) n -> p k n", p=128)
    NTB = TT // 128
    with ExitStack() as st:
        sc = Scope(nc, st, pfx)
        T, P = sc.T, sc.P
        gc = T("gc", [128, D], F32)
        gm = T("gm", [128, D], F32)
        xs = T("xs", [128, D], F32)
        hb = T("hb", [128, D], BF16)
        ss = T("ss", [128, 2], F32)
        rs = T("rs", [128, 2], F32)
        mT = T("mT", [128, 16, TT], BF16)
        h2T = T("h2T", [128, 16, TT], BF16)
        x1 = [T("x1_%d" % i, [128, D], F32) for i in range(NTB)]
        aT = T("aT", [128, 64, TT], BF16)
        wr = [T("w%d" % i, [128, 16, 512], BF16) for i in range(2)]
        tmp = [T("tmp%d" % i, [128, 512], F32) for i in range(2)]
        ob = [T("ob32_%d" % i, [128, 512], F32) for i in range(3)]
        tp = [P("tp%d" % i, [128, 4, 128], BF16) for i in range(2)]
        acc = [P("acc%d" % i, [128, 512], F32) for i in range(4)]
        load_bc(k, nc, gc, g_cat_row, "gc")
        load_bc(k, nc, gm, g_mlp_row, "gm")
        cn = {"w": 0, "ob": 0, "acc": 0, "tmp": 0}

        def wload(src):
            wt, wb = wr[cn["w"] % 2]
            cn["w"] += 1
            k.dma("pool", wb.name, wt[:], src, writes=[wb])
            return wt, wb

        for tt in range(SL // TT):
            for tb in range(NTB):
                r0 = tt * TT + tb * 128
                k.dma("sp", x1[tb][1].name, x1[tb][0][:], x_src[r0:r0 + 128, :], writes=[x1[tb][1]])
                emit_norm_transpose(k, nc, o_s[r0:r0 + 128, :], mT, tb, gc, ident, xs, hb, ss, rs, tp,
                                    [(0, 1024), (1024, 2048)])
            for n in range(4):
                wt, wb = wload(wov[:, :, n * 512:(n + 1) * 512])
                for tb in range(NTB):
                    at, ab = acc[cn["acc"] % 4]
                    cn["acc"] += 1
                    for kc in range(16):
                        k.op("pe", lambda: nc.tensor.matmul(
                            at[:], lhsT=mT[0][:, kc, tb * 128:(tb + 1) * 128], rhs=wt[:, kc, :],
                            start=(kc == 0), stop=(kc == 15)), reads=[mT[1], wb], writes=[ab], inc=(kc == 15))
                    k.op("dve", lambda: nc.vector.tensor_tensor(
                        out=x1[tb][0][:, n * 512:(n + 1) * 512], in0=at[:],
                        in1=x1[tb][0][:, n * 512:(n + 1) * 512], op=ALU.add),
                        reads=[ab, x1[tb][1]], writes=[x1[tb][1]])
            for tb in range(NTB):
                emit_norm_transpose(k, nc, None, h2T, tb, gm, ident, None, hb, ss, rs, tp, [(0, D)], src_sb=x1[tb])
            for fp in range(D_FF // 512):
                wt, wb = wload(wuv[:, :, fp * 512:(fp + 1) * 512])
                for fc in range(4):
                    at, ab = acc[cn["acc"] % 4]
                    cn["acc"] += 1
                    for kc in range(16):
                        k.op("pe", lambda: nc.tensor.matmul(
                            at[:], lhsT=wt[:, kc, fc * 128:(fc + 1) * 128], rhs=h2T[0][:, kc, :],
                            start=(kc == 0), stop=(kc == 15)), reads=[h2T[1], wb], writes=[ab], inc=(kc == 15))
                    tm, tmb = tmp[cn["tmp"] % 2]
                    cn["tmp"] += 1
                    k.op("act", lambda: nc.scalar.activation(out=tm[:], in_=at[:], func=AF.Relu), reads=[ab], writes=[tmb])
                    fidx = fp * 4 + fc
                    if fidx % 2 == 0:
                        k.op("dve", lambda: nc.vector.tensor_tensor(out=aT[0][:, fidx, :], in0=tm[:], in1=tm[:], op=ALU.mult),
                             reads=[tmb], writes=[aT[1]])
                    else:
                        k.op("pool", lambda: nc.gpsimd.tensor_tensor(out=aT[0][:, fidx, :], in0=tm[:], in1=tm[:], op=ALU.mult),
                             reads=[tmb], writes=[aT[1]])
            for n in range(4):
                accs = [acc[(cn["acc"] + i) % 4] for i in range(NTB)]
                cn["acc"] += NTB
                for kp in range(4):
                    wt, wb = wload(w_down[kp * 2048:(kp + 1) * 2048, n * 512:(n + 1) * 512]
                                   .rearrange("(k p) n -> p k n", p=128))
                    for tb in range(NTB):
                        at, ab = accs[tb]
                        for kc in range(16):
                            last = (kp == 3 and kc == 15)
                            k.op("pe", lambda: nc.tensor.matmul(
                                at[:], lhsT=aT[0][:, kp * 16 + kc, tb * 128:(tb + 1) * 128], rhs=wt[:, kc, :],
                                start=(kp == 0 and kc == 0), stop=last),
                                reads=[aT[1], wb], writes=[ab], inc=(kc == 15))
                for tb in range(NTB):
                    at, ab = accs[tb]
                    ot, obb = ob[cn["ob"] % 3]
                    cn["ob"] += 1
                    r0 = tt * TT + tb * 128
                    k.op("dve", lambda: nc.vector.tensor_tensor(
                        out=ot[:], in0=at[:], in1=x1[tb][0][:, n * 512:(n + 1) * 512], op=ALU.add),
                        reads=[ab, x1[tb][1]], writes=[obb])
                    k.dma("sp", obb.name + "_st", x_dst[r0:r0 + 128, n * 512:(n + 1) * 512], ot[:], reads=[obb])
        _barrier(k)


def build_fused(SL=S, depth=DEPTH, phases="123"):
    from contextlib import ExitStack
    nc = bass.Bass("TRN2", target_bir_lowering=False)
    ext = lambda name, shape, dt=F32: nc.dram_tensor(name, shape, dt, kind="ExternalInput").ap()
    x = ext("x", [SL, D])
    w_in = ext("w_in", [depth, D, D_IN])
    w_out = ext("w_out", [depth, D, D])
    w_up = ext("w_up", [depth, D, D_FF])
    w_down = ext("w_down", [depth, D_FF, D])
    g_attn = ext("g_attn", [depth, D])
    g_qk = ext("g_qk", [depth, 128])
    g_cat = ext("g_cat", [depth, D])
    g_mlp = ext("g_mlp", [depth, D])
    sink_bc = ext("sink_bc", [128, depth * 16])
    biasT_d = ext("biasT", [128, 2, 16, 128])
    maskT_d = ext("maskT", [128, 2, 128])
    masks_d = ext("masks", [128, 4, 512])
    ntri_d = ext("ntri", [128, 128])
    ident_d = ext("ident", [128, 128])
    xo = nc.dram_tensor("xo", [SL, D], F32, kind="ExternalOutput").ap()
    xa = nc.dram_tensor("x_scratch", [SL, D], F32).ap()
    o_s = nc.dram_tensor("o_scratch", [SL, D], F32).ap()
    scr = {
        "qaT": nc.dram_tensor("qaT_scr", [8, 128, SL], BF16).ap(),
        "kaT2": nc.dram_tensor("kaT2_scr", [4, 128, SL], BF16).ap(),
        "va": nc.dram_tensor("va_scr", [SL, 256], BF16).ap(),
        "qbT": nc.dram_tensor("qbT_scr", [8, 128, SL], BF16).ap(),
        "kbT": nc.dram_tensor("kbT_scr", [8, 128, SL], BF16).ap(),
        "vb": nc.dram_tensor("vb_scr", [SL, 1024], BF16).ap(),
    }
    with ExitStack() as st0:
        k = K(nc, st0)
        ident32 = (st0.enter_context(nc.sbuf_tensor("ident32", [128, 128], F32)), Buf("ident32"))
        ident = (st0.enter_context(nc.sbuf_tensor("identb", [128, 128], BF16)), Buf("identb"))
        with nc.Block():
            k.dma("sp", "ident32", ident32[0][:], ident_d, writes=[ident32[1]])
            k.op("dve", lambda: nc.vector.tensor_copy(out=ident[0][:], in_=ident32[0][:]),
                 reads=[ident32[1]], writes=[ident[1]])
            for l in range(depth):
                x_src = x if l == 0 else xa
                x_dst = xo if l == depth - 1 else xa
                if "1" in phases:
                    emit_P1(k, nc, "L%dp1_" % l, SL, x_src, w_in[l], g_attn[l:l + 1, :], g_qk[l:l + 1, :], ident, scr)
                for gi in (range(4) if "2" in phases else []):
                    emit_P2(k, nc, "L%dp2g%d" % (l, gi), SL, gi, scr, biasT_d, maskT_d,
                            sink_bc, l * 16 + 4 * gi, masks_d, ntri_d, o_s,
                            parts="".join(ch for ch in phases if ch in "ab") or "ab")
                if "3" in phases:
                    emit_P3(k, nc, "L%dp3_" % l, SL, x_src, x_dst, o_s, w_out[l], w_up[l], w_down[l],
                            g_cat[l:l + 1, :], g_mlp[l:l + 1, :], ident)
            _barrier(k)
    return nc


def _t5_bucket(dist):
    max_exact = 16
    d = np.maximum(dist, 0)
    ratio = np.maximum(d, 1).astype(np.float32) / max_exact
    large = max_exact + (np.log(ratio) / math.log(128 / max_exact) * (32 - max_exact)).astype(np.int32)
    large = np.minimum(large, 31)
    return np.where(d < max_exact, d, large).astype(np.int32)


def l2_constants():
    s = np.arange(128)[:, None, None]
    j = np.arange(2)[None, :, None]
    q = np.arange(128)[None, None, :]
    dist = q + 128 - (j * 128 + s)
    valid = (dist >= 0) & (dist < 128)
    maskT = np.where(valid, 0.0, NEG).astype(np.float32)
    bucket = _t5_bucket(dist)
    sa = (np.arange(4)[None, :, None] * 128 + np.arange(128)[:, None, None])
    masks = (sa < np.arange(512)[None, None, :]).astype(np.float32)
    ntri = -(np.arange(128)[:, None] >= np.arange(128)[None, :]).astype(np.float32)
    return bucket, maskT, masks, ntri


_PROG = {}


def make_inputs(x_seq, norm_attn_g, w_in, q_norm_g, k_norm_g, sinks, rel_bias, swa_out_g, sb_out_g, w_out,
                norm_mlp_g, w_up, w_down, depth):
    f32 = np.float32
    c = lambda a: np.ascontiguousarray(np.asarray(a, f32))
    bucket, maskT, masks, ntri = l2_constants()
    biasT = c(np.asarray(rel_bias, f32)[bucket].transpose(0, 1, 3, 2))
    sink_bc = c(np.broadcast_to(np.asarray(sinks, f32)[:depth].reshape(1, depth * 16), (128, depth * 16)))
    shared = {
        "w_in": c(w_in[:depth]), "w_out": c(w_out[:depth]), "w_up": c(w_up[:depth]), "w_down": c(w_down[:depth]),
        "g_attn": c(norm_attn_g[:depth]),
        "g_qk": c(np.concatenate([np.asarray(q_norm_g, f32)[:depth], np.asarray(k_norm_g, f32)[:depth]], 1)),
        "g_cat": c(np.concatenate([np.asarray(swa_out_g, f32)[:depth], np.asarray(sb_out_g, f32)[:depth]], 1)),
        "g_mlp": c(norm_mlp_g[:depth]),
        "sink_bc": sink_bc, "biasT": biasT, "maskT": maskT, "masks": masks, "ntri": ntri,
        "ident": np.eye(128, dtype=f32)}
    return [dict(shared, x=c(xs)) for xs in x_seq]


def kernel(x, norm_attn_g, w_in, q_norm_g, k_norm_g, sinks, rel_bias, swa_out_g, sb_out_g, w_out,
           norm_mlp_g, w_up, w_down):
    x = np.asarray(x, np.float32)
    nb, sl = x.shape[0], x.shape[1]
    key = (sl, DEPTH)
    if key not in _PROG:
        _PROG[key] = build_fused(sl, DEPTH)
    in_maps = make_inputs([x[b] for b in range(nb)], norm_attn_g, w_in, q_norm_g, k_norm_g, sinks, rel_bias,
                          swa_out_g, sb_out_g, w_out, norm_mlp_g, w_up, w_down, DEPTH)
    res = run_bass_kernel_spmd(_PROG[key], in_maps, core_ids=list(range(nb)))
    return np.stack([res.results[b]["xo"] for b in range(nb)], 0).astype(np.float32)
```

```python
import math
import numpy as np
import ml_dtypes
import concourse.bass as bass
import concourse.mybir as mybir
from concourse.bass_utils import run_bass_kernel_spmd

F32 = mybir.dt.float32
BF16 = mybir.dt.bfloat16
AF = mybir.ActivationFunctionType
ALU = mybir.AluOpType
AX = mybir.AxisListType

D = 2048
B = 2
S = 8192
DEPTH = 4
DH = 64
NQA = 16
NKA = 4
NSB = 16
D_IN = 4608
D_FF = 8192
EPS = 1e-6
NCORES = 8
TOK = (B * S) // NCORES
TT = 512
NEG = -30000.0


class Buf:
    __slots__ = ("name", "writer", "readers")

    def __init__(self, name):
        self.name = name
        self.writer = None
        self.readers = {}


class K:
    def __init__(self, nc, stack):
        self.nc = nc
        self.eng = {"pe": nc.tensor, "act": nc.scalar, "dve": nc.vector, "pool": nc.gpsimd, "sp": nc.sync}
        self.sem = {}
        self.cnt = {}
        for e in ("pe", "act", "dve", "pool"):
            self.sem[e] = stack.enter_context(nc.semaphore("s_" + e))
            self.cnt[e] = 0
        self.seen = {e: {} for e in self.eng}
        self.stack = stack
        self.dma_sems = {}
        self.dma_cnt = {}
        self.semobj = {e: self.sem[e] for e in self.sem}
        self.out_tokens = []

    def dma_sem(self, key):
        if key not in self.dma_sems:
            self.dma_sems[key] = self.stack.enter_context(self.nc.semaphore("d_" + key))
            self.dma_cnt[key] = 0
            self.semobj["d_" + key] = self.dma_sems[key]
        return self.dma_sems[key]

    def _wait(self, e, tok):
        if tok is None:
            return
        key, val = tok
        if key == e and e == "pe":
            return
        if self.seen[e].get(key, 0) >= val:
            return
        self.eng[e].wait_ge(self.semobj[key], val)
        self.seen[e][key] = val

    def _deps(self, e, reads, writes):
        for b in reads:
            self._wait(e, b.writer)
        for b in writes:
            self._wait(e, b.writer)
            for t in list(b.readers.items()):
                if t[0] != e:
                    self._wait(e, t)

    def _mark(self, tok, reads, writes):
        for b in reads:
            if b.readers.get(tok[0], 0) < tok[1]:
                b.readers[tok[0]] = tok[1]
        for b in writes:
            b.writer = tok
            b.readers = {}

    def op(self, e, fn, reads=(), writes=(), inc=True):
        self._deps(e, reads, writes)
        ins = fn()
        if inc:
            self.cnt[e] += 1
            ins.then_inc(self.sem[e], 1)
            tok = (e, self.cnt[e])
        else:
            tok = (e, self.cnt[e] + 1)
        self._mark(tok, reads, writes)
        return tok

    def dma(self, q, key, out, in_, reads=(), writes=()):
        sem = self.dma_sem(key)
        self._deps(q, reads, writes)
        self.eng[q].dma_start(out=out, in_=in_).then_inc(sem, 16)
        self.dma_cnt[key] += 16
        tok = ("d_" + key, self.dma_cnt[key])
        self._mark(tok, reads, writes)
        return tok

    def finish(self, e="sp"):
        last = {}
        for t in self.out_tokens:
            last[t[0]] = max(last.get(t[0], 0), t[1])
        for t in last.items():
            self._wait(e, t)
        for x in ("pe", "act", "dve", "pool"):
            if self.cnt[x]:
                self._wait(e, (x, self.cnt[x]))


def _bc(ap, shape):
    return ap.to_broadcast(list(shape))


def emit_norm_transpose(k, nc, src_dram_rows, hT, tb, g_bc, ident, xs, hb, ss, rs, tp, halves, junk=None,
                        keep=None, src_sb=None):
    if src_sb is not None:
        xt, xb = src_sb
    else:
        xt, xb = xs
        k.dma("sp", xb.name, xt[:], src_dram_rows, writes=[xb])
    if junk is None:
        junk = hb
    k.op("dve", lambda: nc.vector.memset(ss[0][:], 0.0), writes=[ss[1]])
    for i, (lo, hi) in enumerate(halves):
        k.op("act", lambda lo=lo, hi=hi, i=i: nc.scalar.activation(
            out=junk[0][:, 0:hi - lo], in_=xt[:, lo:hi], func=AF.Square, accum_out=ss[0][:, i:i + 1]),
            reads=[xb], writes=[junk[1], ss[1]])
    nh = len(halves)
    w = halves[0][1] - halves[0][0]
    k.op("dve", lambda: nc.vector.tensor_scalar(out=rs[0][:, 0:nh], in0=ss[0][:, 0:nh], scalar1=1.0 / w,
                                                scalar2=EPS, op0=ALU.mult, op1=ALU.add),
         reads=[ss[1]], writes=[rs[1]])
    k.op("act", lambda: nc.scalar.activation(out=rs[0][:, 0:nh], in_=rs[0][:, 0:nh], func=AF.Sqrt),
         reads=[rs[1]], writes=[rs[1]])
    k.op("dve", lambda: nc.vector.reciprocal(out=rs[0][:, 0:nh], in_=rs[0][:, 0:nh]),
         reads=[rs[1]], writes=[rs[1]])
    for i, (lo, hi) in enumerate(halves):
        k.op("dve", lambda lo=lo, hi=hi, i=i: nc.vector.scalar_tensor_tensor(
            out=hb[0][:, lo:hi], in0=xt[:, lo:hi], scalar=rs[0][:, i:i + 1], in1=g_bc[0][:, lo:hi],
            op0=ALU.mult, op1=ALU.mult), reads=[xb, rs[1], g_bc[1]], writes=[hb[1]])
    if keep is not None:
        k.op("pool", lambda: nc.gpsimd.tensor_copy(out=keep[0][:], in_=xt[:]), reads=[xb], writes=[keep[1]])
    for cg in range(4):
        tpt, tpb = tp[cg % len(tp)]
        for j in range(4):
            c = cg * 4 + j
            k.op("pe", lambda c=c, j=j, tpt=tpt: nc.tensor.transpose(
                out=tpt[:, j, :], in_=hb[0][:, c * 128:(c + 1) * 128], identity=ident[0][:]),
                reads=[hb[1], ident[1]], writes=[tpb], inc=(j == 3))
        eng = "act" if cg % 2 == 0 else "dve"
        if eng == "act":
            k.op("act", lambda cg=cg, tpt=tpt: nc.scalar.copy(
                out=hT[0][:, cg * 4:cg * 4 + 4, tb * 128:(tb + 1) * 128], in_=tpt[:]),
                reads=[tpb], writes=[hT[1]])
        else:
            k.op("dve", lambda cg=cg, tpt=tpt: nc.vector.tensor_copy(
                out=hT[0][:, cg * 4:cg * 4 + 4, tb * 128:(tb + 1) * 128], in_=tpt[:]),
                reads=[tpb], writes=[hT[1]])


def load_bc(k, nc, tile_buf, dram_row_ap, key):
    t, b = tile_buf
    k.dma("sp", key, t[:], dram_row_ap.partition_broadcast(128), writes=[b])


def _barrier(k):
    toks = [(e, k.cnt[e]) for e in ("pe", "act", "dve", "pool") if k.cnt[e]]
    toks += [("d_" + key, c) for key, c in k.dma_cnt.items() if c]
    for e in ("pe", "act", "dve", "pool", "sp"):
        for t in toks:
            k._wait(e, t)


class Scope:
    def __init__(self, nc, st, pfx):
        self.nc, self.st, self.pfx = nc, st, pfx

    def T(self, name, shape, dt):
        return (self.st.enter_context(self.nc.sbuf_tensor(self.pfx + name, shape, dt)), Buf(name))

    def P(self, name, shape, dt):
        return (self.st.enter_context(self.nc.psum_tensor(self.pfx + name, shape, dt)), Buf(name))


def emit_P1(k, nc, pfx, SL, x_src, wc_in, g_attn_row, g_qk_row, ident, scr):
    from contextlib import ExitStack
    wv = wc_in[0].rearrange("(k p) n -> p k n", p=128)
    scale = 1.0 / math.sqrt(DH)
    qaT_v = scr["qaT"].rearrange("h p s -> p h s")
    kaT_v = scr["kaT2"].rearrange("h p s -> p h s")
    with ExitStack() as st:
        sc = Scope(nc, st, pfx)
        T, P = sc.T, sc.P
        g_bc = T("g_bc", [128, D], F32)
        gq = T("gq", [128, 128], F32)
        xs = [T("xs%d" % i, [128, D], F32) for i in range(2)]
        hb = [T("hb%d" % i, [128, D], BF16) for i in range(2)]
        ss = [T("ss%d" % i, [128, 2], F32) for i in range(2)]
        rs = [T("rs%d" % i, [128, 2], F32) for i in range(2)]
        hT = [T("hT%d" % i, [128, 16, TT], BF16) for i in range(2)]
        wr = [T("w%d" % i, [128, 16, 512], BF16) for i in range(2)]
        sq = [T("sq%d" % i, [128, 512], F32) for i in range(2)]
        qs = [T("qs%d" % i, [128, 8], F32) for i in range(2)]
        qn = [T("qn%d" % i, [128, 512], F32) for i in range(2)]
        ob = [T("ob%d" % i, [128, 512], BF16) for i in range(3)]
        kk = [T("kk%d" % i, [128, 4, 2, 64], BF16) for i in range(2)]
        qst = [T("qst%d" % i, [128, 4, 128], BF16) for i in range(2)]
        tp = [P("tp%d" % i, [128, 4, 128], BF16) for i in range(2)]
        acc = [P("acc%d" % i, [128, 512], F32) for i in range(4)]
        load_bc(k, nc, g_bc, g_attn_row, "g_bc")
        load_bc(k, nc, gq, g_qk_row, "gq")
        cn = {"w": 0, "ob": 0, "acc": 0, "tp": 0, "qst": 0, "kk": 0}

        def nxt(ring, key):
            r = ring[cn[key] % len(ring)]
            cn[key] += 1
            return r

        def transposes_out(src_fn, src_buf, dst_ap):
            tpt, tpb = nxt(tp, "tp")
            for j in range(4):
                k.op("pe", lambda j=j: nc.tensor.transpose(out=tpt[:, j, :], in_=src_fn(j), identity=ident[0][:]),
                     reads=[src_buf, ident[1]], writes=[tpb], inc=(j == 3))
            qt_, qtb = nxt(qst, "qst")
            k.op("dve", lambda: nc.vector.tensor_copy(out=qt_[:], in_=tpt[:]), reads=[tpb], writes=[qtb])
            k.dma("sp", qtb.name + "_st", dst_ap, qt_[:], reads=[qtb])

        for tt in range(SL // TT):
            hTt = hT[tt % 2]
            t0 = tt * TT
            for tb in range(TT // 128):
                i = (tt * 4 + tb) % 2
                r0 = t0 + tb * 128
                emit_norm_transpose(k, nc, x_src[r0:r0 + 128, :], hTt, tb, g_bc, ident, xs[i], hb[i],
                                    ss[i], rs[i], tp, [(0, D)])
            for pc in range(9):
                wt, wb = nxt(wr, "w")
                k.dma("sp", wb.name, wt[:], wv[:, :, pc * 512:(pc + 1) * 512], reads=[wc_in[1]], writes=[wb])
                if pc in (3, 4, 5, 6):
                    dst = scr["qbT"] if pc < 5 else scr["kbT"]
                    for fc in range(4):
                        at, ab = nxt(acc, "acc")
                        for kc in range(16):
                            k.op("pe", lambda kc=kc, fc=fc: nc.tensor.matmul(
                                at[:], lhsT=wt[:, kc, fc * 128:(fc + 1) * 128], rhs=hTt[0][:, kc, :],
                                start=(kc == 0), stop=(kc == 15)),
                                reads=[hTt[1], wb], writes=[ab], inc=(kc == 15))
                        ot, obb = nxt(ob, "ob")
                        if pc < 5:
                            k.op("act", lambda: nc.scalar.mul(ot[:], at[:], scale), reads=[ab], writes=[obb])
                        elif fc % 2 == 0:
                            k.op("act", lambda: nc.scalar.copy(out=ot[:], in_=at[:]), reads=[ab], writes=[obb])
                        else:
                            k.op("dve", lambda: nc.vector.tensor_copy(out=ot[:], in_=at[:]), reads=[ab], writes=[obb])
                        hpi = ((pc - 3) % 2) * 4 + fc
                        k.dma("sp", obb.name + "_st", dst[hpi, :, t0:t0 + TT], ot[:], reads=[obb])
                    continue
                for tb in range(TT // 128):
                    at, ab = nxt(acc, "acc")
                    for kc in range(16):
                        k.op("pe", lambda kc=kc, tb=tb: nc.tensor.matmul(
                            at[:], lhsT=hTt[0][:, kc, tb * 128:(tb + 1) * 128], rhs=wt[:, kc, :],
                            start=(kc == 0), stop=(kc == 15)),
                            reads=[hTt[1], wb], writes=[ab], inc=(kc == 15))
                    ot, obb = nxt(ob, "ob")
                    r0 = t0 + tb * 128
                    if pc in (0, 1, 2):
                        j = (tt * 36 + pc * 4 + tb) % 2
                        nq = 512 if pc < 2 else 256
                        nh = nq // DH
                        goff = 0 if pc < 2 else 64
                        sc_ = scale if pc < 2 else 1.0
                        k.op("act", lambda: nc.scalar.activation(out=sq[j][0][:, 0:nq], in_=at[:, 0:nq], func=AF.Square),
                             reads=[ab], writes=[sq[j][1]])
                        k.op("dve", lambda: nc.vector.tensor_reduce(
                            out=qs[j][0][:, 0:nh], in_=sq[j][0][:, 0:nq].rearrange("p (h d) -> p h d", d=DH),
                            axis=AX.X, op=ALU.add), reads=[sq[j][1]], writes=[qs[j][1]])
                        k.op("dve", lambda: nc.vector.tensor_scalar(
                            out=qs[j][0][:, 0:nh], in0=qs[j][0][:, 0:nh], scalar1=1.0 / DH, scalar2=EPS,
                            op0=ALU.mult, op1=ALU.add), reads=[qs[j][1]], writes=[qs[j][1]])
                        k.op("act", lambda: nc.scalar.activation(out=qs[j][0][:, 0:nh], in_=qs[j][0][:, 0:nh], func=AF.Sqrt),
                             reads=[qs[j][1]], writes=[qs[j][1]])
                        k.op("dve", lambda: nc.vector.reciprocal(out=qs[j][0][:, 0:nh], in_=qs[j][0][:, 0:nh]),
                             reads=[qs[j][1]], writes=[qs[j][1]])
                        k.op("dve", lambda: nc.vector.tensor_tensor(
                            out=qn[j][0][:, 0:nq].rearrange("p (h d) -> p h d", d=DH),
                            in0=at[:, 0:nq].rearrange("p (h d) -> p h d", d=DH),
                            in1=_bc(qs[j][0][:, 0:nh].unsqueeze(2), [128, nh, DH]), op=ALU.mult),
                            reads=[ab, qs[j][1]], writes=[qn[j][1]])
                        k.op("dve", lambda: nc.vector.scalar_tensor_tensor(
                            out=ot[:, 0:nq].rearrange("p (h d) -> p h d", d=DH),
                            in0=qn[j][0][:, 0:nq].rearrange("p (h d) -> p h d", d=DH), scalar=sc_,
                            in1=_bc(gq[0][:, goff:goff + DH].unsqueeze(1), [128, nh, DH]),
                            op0=ALU.mult, op1=ALU.mult), reads=[qn[j][1], gq[1]], writes=[obb])
                        if pc < 2:
                            transposes_out(lambda jj: ot[:, jj * 128:(jj + 1) * 128], obb,
                                           qaT_v[:, pc * 4:pc * 4 + 4, r0:r0 + 128])
                        else:
                            k.op("act", lambda: nc.scalar.copy(out=ot[:, 256:512], in_=at[:, 256:512]),
                                 reads=[ab], writes=[obb])
                            k.dma("sp", obb.name + "_st", scr["va"][r0:r0 + 128, :], ot[:, 256:512], reads=[obb])
                            kt, kkb = nxt(kk, "kk")
                            k.op("dve", lambda: nc.vector.tensor_copy(
                                out=kt[:, :, 0, :], in_=ot[:, 0:256].rearrange("p (g d) -> p g d", d=DH)),
                                reads=[obb], writes=[kkb])
                            k.op("pool", lambda: nc.gpsimd.tensor_copy(
                                out=kt[:, :, 1, :], in_=ot[:, 0:256].rearrange("p (g d) -> p g d", d=DH)),
                                reads=[obb], writes=[kkb])
                            transposes_out(lambda jj: kt[:, jj].rearrange("p a d -> p (a d)"), kkb,
                                           kaT_v[:, 0:4, r0:r0 + 128])
                    else:
                        if tb % 2 == 0:
                            k.op("act", lambda: nc.scalar.copy(out=ot[:], in_=at[:]), reads=[ab], writes=[obb])
                        else:
                            k.op("dve", lambda: nc.vector.tensor_copy(out=ot[:], in_=at[:]), reads=[ab], writes=[obb])
                        k.dma("sp", obb.name + "_st", scr["vb"][r0:r0 + 128, (pc - 7) * 512:(pc - 6) * 512], ot[:],
                              reads=[obb])
        _barrier(k)


def emit_P2(k, nc, pfx, SL, gi, scr, biasT_d, maskT_d, sink_all, sink_c0, masks_d, ntri_d, o_s, parts="ab"):
    from contextlib import ExitStack
    NKB = SL // 128
    NQT = SL // 512
    def _swa():
        with ExitStack() as st:
            sc = Scope(nc, st, pfx + "a_")
            T, P = sc.T, sc.P
            qaT = T("qaT_s", [128, 2, SL], BF16)
            kaT = T("kaT_s", [128, SL], BF16)
            va1 = T("va1", [128, NKB, 72], BF16)
            biasT = T("biasT_s", [128, 2, 16, 128], F32)
            maskT = T("maskT_s", [128, 2, 128], F32)
            sinkt = T("sinkt", [128, sink_all.shape[1]], F32)
            esink = T("esink", [128, sink_all.shape[1]], F32)
            stt = [T("stt%d" % i, [128, 4, 128], F32) for i in range(2)]
            pT = [T("pT%d" % i, [128, 4, 128], BF16) for i in range(4)]
            den = [T("den%d" % i, [128, 4], F32) for i in range(2)]
            oa = [T("oa%d" % i, [128, 4, 64], F32) for i in range(2)]
            sps = [[P("sps%d_%d" % (i, par), [128, 2, 128], F32) for par in range(2)] for i in range(2)]
            ops = [P("ops%d" % i, [128, 4, 65], F32) for i in range(2)]
            k.dma("sp", "qaT_s", qaT[0][:], scr["qaT"][2 * gi:2 * gi + 2].rearrange("h p s -> p h s"), writes=[qaT[1]])
            k.dma("sp", "kaT_s", kaT[0][:], scr["kaT2"][gi], writes=[kaT[1]])
            k.op("dve", lambda: nc.vector.memset(va1[0][:], 1.0), writes=[va1[1]])
            for c0 in range(0, NKB, 8):
                k.dma("sp", "va1", va1[0][:, c0:c0 + 8, 0:64],
                      scr["va"][c0 * 128:(c0 + 8) * 128, gi * 64:(gi + 1) * 64]
                      .rearrange("(kb p) d -> p kb d", p=128), writes=[va1[1]])
            k.dma("sp", "biasT_s", biasT[0][:], biasT_d, writes=[biasT[1]])
            k.dma("sp", "maskT_s", maskT[0][:], maskT_d, writes=[maskT[1]])
            k.dma("sp", "sinkt", sinkt[0][:], sink_all, writes=[sinkt[1]])
            for j in range(2):
                k.op("dve", lambda: nc.vector.tensor_tensor(
                    out=biasT[0][:, j], in0=biasT[0][:, j],
                    in1=_bc(maskT[0][:, j].unsqueeze(1), [128, 16, 128]), op=ALU.add),
                    reads=[biasT[1], maskT[1]], writes=[biasT[1]])
            k.op("act", lambda: nc.scalar.activation(out=esink[0][:], in_=sinkt[0][:], func=AF.Exp),
                 reads=[sinkt[1]], writes=[esink[1]])
            nst = 0
            npt = 0
            for qb in range(NKB):
                js = [1] if qb == 0 else [0, 1]
                pts = []
                for j in js:
                    kb = qb - 1 + j
                    sp2 = sps[nst % 2]
                    stt_t, stt_b = stt[nst % 2]
                    nst += 1
                    for h in range(4):
                        hp, par = h // 2, h % 2
                        pb = 64 * par
                        spt, spb = sp2[par]
                        k.op("pe", lambda: nc.tensor.matmul(
                            spt[:, hp, :], lhsT=kaT[0][pb:pb + 64, kb * 128:(kb + 1) * 128],
                            rhs=qaT[0][pb:pb + 64, hp, qb * 128:(qb + 1) * 128], start=True, stop=True),
                            reads=[kaT[1], qaT[1]], writes=[spb], inc=(h >= 2))
                    for par in range(2):
                        spt, spb = sp2[par]
                        k.op("dve", lambda: nc.vector.tensor_tensor(
                            out=stt_t[:].rearrange("p (a b) q -> p a b q", b=2)[:, :, par, :], in0=spt[:],
                            in1=biasT[0][:, j, 4 * gi:4 * gi + 4, :].rearrange("p (a b) q -> p a b q", b=2)[:, :, par, :],
                            op=ALU.add), reads=[spb, biasT[1]], writes=[stt_b])
                    ptt, ptb = pT[npt % 4]
                    npt += 1
                    k.op("act", lambda: nc.scalar.activation(out=ptt[:], in_=stt_t[:], func=AF.Exp),
                         reads=[stt_b], writes=[ptb])
                    pts.append((ptt, ptb, kb))
                opt, opb = ops[qb % 2]
                for h in range(4):
                    for idx, (ptt, ptb, kb) in enumerate(pts):
                        k.op("pe", lambda: nc.tensor.matmul(
                            opt[:, h, :], lhsT=ptt[:, h, :], rhs=va1[0][:, kb, 0:65],
                            start=(idx == 0), stop=(idx == len(pts) - 1)),
                            reads=[ptb, va1[1]], writes=[opb], inc=(h == 3 and idx == len(pts) - 1))
                dn, dnb = den[qb % 2]
                oat, oab = oa[qb % 2]
                k.op("dve", lambda: nc.vector.tensor_tensor(out=dn[:], in0=opt[:, :, 64], in1=esink[0][:, sink_c0:sink_c0 + 4], op=ALU.add),
                     reads=[opb, esink[1]], writes=[dnb])
                k.op("dve", lambda: nc.vector.reciprocal(out=dn[:], in_=dn[:]), reads=[dnb], writes=[dnb])
                k.op("dve", lambda: nc.vector.tensor_tensor(
                    out=oat[:], in0=opt[:, :, 0:64], in1=_bc(dn[:].unsqueeze(2), [128, 4, 64]), op=ALU.mult),
                    reads=[opb, dnb], writes=[oab])
                k.dma("sp", oab.name + "_st", o_s[qb * 128:(qb + 1) * 128, gi * 256:(gi + 1) * 256],
                      oat[:].rearrange("p h d -> p (h d)"), reads=[oab])
            _barrier(k)

    def _sb():
        with ExitStack() as st:
            sc = Scope(nc, st, pfx + "b_")
            T, P = sc.T, sc.P
            qbT = T("qbT_s", [128, 2, SL], BF16)
            kbT = T("kbT_s", [128, 2, SL], BF16)
            vb = T("vb_s", [128, NKB, 256], BF16)
            m32 = T("m32", [128, 4, 512], F32)
            masks = T("masks_s", [128, 4, 512], BF16)
            t32 = T("t32", [128, 128], F32)
            ntri = T("ntri_s", [128, 128], BF16)
            nones = T("nones_s", [128, 128], BF16)
            e32 = [T("e32_%d" % i, [128, 512], F32) for i in range(2)]
            spr = [T("sp_%d" % i, [128, 512], BF16) for i in range(3)]
            wr_ = [T("wv_%d" % i, [128, 512], BF16) for i in range(2)]
            R32 = T("R32", [128, 512], F32)
            Rb = [T("Rb_%d" % i, [128, 512], BF16) for i in range(2)]
            ost = [T("ost_%d" % i, [128, 4, 64], F32) for i in range(2)]
            zps = [P("zps%d" % i, [128, 512], F32) for i in range(2)]
            eps_ = [P("eps%d" % i, [128, 512], F32) for i in range(2)]
            oacc = [P("oacc%d" % i, [128, 4, 64], F32) for i in range(2)]
            k.dma("sp", "qbT_s", qbT[0][:], scr["qbT"][2 * gi:2 * gi + 2].rearrange("h p s -> p h s"), writes=[qbT[1]])
            k.dma("sp", "kbT_s", kbT[0][:], scr["kbT"][2 * gi:2 * gi + 2].rearrange("h p s -> p h s"), writes=[kbT[1]])
            for c0 in range(0, NKB, 8):
                k.dma("sp", "vb_s", vb[0][:, c0:c0 + 8, :],
                      scr["vb"][c0 * 128:(c0 + 8) * 128, gi * 256:(gi + 1) * 256]
                      .rearrange("(kb p) f -> p kb f", p=128), writes=[vb[1]])
            k.dma("sp", "m32", m32[0][:], masks_d, writes=[m32[1]])
            k.dma("sp", "t32", t32[0][:], ntri_d, writes=[t32[1]])
            k.op("dve", lambda: nc.vector.tensor_copy(out=masks[0][:], in_=m32[0][:]), reads=[m32[1]], writes=[masks[1]])
            k.op("dve", lambda: nc.vector.tensor_copy(out=ntri[0][:], in_=t32[0][:]), reads=[t32[1]], writes=[ntri[1]])
            k.op("dve", lambda: nc.vector.memset(nones[0][:], -1.0), writes=[nones[1]])
            units = [(h, qt, kb) for h in range(4) for qt in range(NQT) for kb in range(4 * qt + 3, -1, -1)]
            state = {}

            def stageA(u, n):
                h, qt, kb = u
                hp, pb = h // 2, 64 * (h % 2)
                zt, zb = zps[n % 2]
                et, eb = e32[n % 2]
                st_, sbb = spr[n % 3]
                k.op("pe", lambda: nc.tensor.matmul(
                    zt[:], lhsT=kbT[0][pb:pb + 64, hp, kb * 128:(kb + 1) * 128],
                    rhs=qbT[0][pb:pb + 64, hp, qt * 512:(qt + 1) * 512], start=True, stop=True),
                    reads=[kbT[1], qbT[1]], writes=[zb])
                k.op("act", lambda: nc.scalar.activation(out=et[:], in_=zt[:], func=AF.Exp), reads=[zb], writes=[eb])
                k.op("act", lambda: nc.scalar.activation(out=st_[:], in_=et[:], func=AF.Ln, bias=1.0),
                     reads=[eb], writes=[sbb])
                i = kb - 4 * qt
                if i >= 0:
                    k.op("pool", lambda: nc.gpsimd.tensor_tensor(out=st_[:], in0=st_[:], in1=masks[0][:, i, :], op=ALU.mult),
                         reads=[sbb, masks[1]], writes=[sbb])
                state[n] = (st_, sbb)

            def stageB(u, n):
                h, qt, kb = u
                hp, pb = h // 2, 64 * (h % 2)
                st_, sbb = state.pop(n)
                first = (kb == 4 * qt + 3)
                et, eb = eps_[n % 2]
                wt, wb = wr_[n % 2]
                ot, obuf = oacc[(h * NQT + qt) % 2]
                rbt, rbb = Rb[n % 2]
                k.op("pe", lambda: nc.tensor.matmul(
                    et[:], lhsT=kbT[0][pb:pb + 64, hp, kb * 128:(kb + 1) * 128],
                    rhs=qbT[0][pb:pb + 64, hp, qt * 512:(qt + 1) * 512], start=True, stop=False),
                    reads=[kbT[1], qbT[1]], writes=[eb], inc=False)
                k.op("pe", lambda: nc.tensor.matmul(et[:], lhsT=ntri[0][:], rhs=st_[:], start=False, stop=first),
                     reads=[ntri[1], sbb], writes=[eb], inc=first)
                if not first:
                    k.op("pe", lambda: nc.tensor.matmul(et[:], lhsT=nones[0][:], rhs=rbt[:], start=False, stop=True),
                         reads=[nones[1], rbb], writes=[eb])
                k.op("act", lambda: nc.scalar.activation(out=wt[:], in_=et[:], func=AF.Exp), reads=[eb], writes=[wb])
                i = kb - 4 * qt
                if i >= 0:
                    k.op("dve", lambda: nc.vector.tensor_tensor(out=wt[:], in0=wt[:], in1=masks[0][:, i, :], op=ALU.mult),
                         reads=[wb, masks[1]], writes=[wb])
                for sub in range(4):
                    k.op("pe", lambda: nc.tensor.matmul(
                        ot[:, sub, :], lhsT=wt[:, sub * 128:(sub + 1) * 128], rhs=vb[0][:, kb, h * 64:(h + 1) * 64],
                        start=first, stop=(kb == 0)), reads=[vb[1], wb], writes=[obuf], inc=(sub == 3))
                if kb > 0:
                    nrb, nrbb = Rb[(n + 1) % 2]
                    if first:
                        k.op("pool", lambda: nc.gpsimd.tensor_copy(out=R32[0][:], in_=st_[:]), reads=[sbb], writes=[R32[1]])
                    else:
                        k.op("pool", lambda: nc.gpsimd.tensor_tensor(out=R32[0][:], in0=R32[0][:], in1=st_[:], op=ALU.add),
                             reads=[sbb, R32[1]], writes=[R32[1]])
                    k.op("pool", lambda: nc.gpsimd.tensor_copy(out=nrb[:], in_=R32[0][:]), reads=[R32[1]], writes=[nrbb])
                else:
                    o_t, o_b = ost[(h * NQT + qt) % 2]
                    k.op("dve", lambda: nc.vector.tensor_copy(out=o_t[:], in_=ot[:]), reads=[obuf], writes=[o_b])
                    c0 = 1024 + (4 * gi + h) * 64
                    k.dma("sp", o_b.name + "_st",
                          o_s[qt * 512:(qt + 1) * 512, c0:c0 + 64].rearrange("(sub p) d -> p sub d", p=128),
                          o_t[:], reads=[o_b])

            stageA(units[0], 0)
            for n, u in enumerate(units):
                if n + 1 < len(units):
                    stageA(units[n + 1], n + 1)
                stageB(u, n)
            _barrier(k)

    if "a" in parts:
        _swa()
    if "b" in parts:
        _sb()


def emit_P3(k, nc, pfx, SL, x_src, x_dst, o_s, wc_out, wc_up, wc_down, g_cat_row, g_mlp_row, ident):
    from contextlib import ExitStack
    wov = wc_out[0].rearrange("(k p) n -> p k n", p=128)
    wuv = wc_up[0].rearrange("(k p) n -> p k n", p=128)
    w_down = wc_down[0]
    NTB = TT // 128
    with ExitStack() as st:
        sc = Scope(nc, st, pfx)
        T, P = sc.T, sc.P
        gc = T("gc", [128, D], F32)
        gm = T("gm", [128, D], F32)
        xs = T("xs", [128, D], F32)
        hb = T("hb", [128, D], BF16)
        ss = T("ss", [128, 2], F32)
        rs = T("rs", [128, 2], F32)
        mT = T("mT", [128, 16, TT], BF16)
        h2T = T("h2T", [128, 16, TT], BF16)
        x1 = [T("x1_%d" % i, [128, D], F32) for i in range(NTB)]
        aT = T("aT", [128, 64, TT], BF16)
        wr = [T("w%d" % i, [128, 16, 512], BF16) for i in range(2)]
        tmp = [T("tmp%d" % i, [128, 512], F32) for i in range(2)]
        ob = [T("ob32_%d" % i, [128, 512], F32) for i in range(3)]
        tp = [P("tp%d" % i, [128, 4, 128], BF16) for i in range(2)]
        acc = [P("acc%d" % i, [128, 512], F32) for i in range(4)]
        load_bc(k, nc, gc, g_cat_row, "gc")
        load_bc(k, nc, gm, g_mlp_row, "gm")
        cn = {"w": 0, "ob": 0, "acc": 0, "tmp": 0}

        def wload(src, cbuf):
            wt, wb = wr[cn["w"] % 2]
            cn["w"] += 1
            k.dma("sp", wb.name, wt[:], src, reads=[cbuf], writes=[wb])
            return wt, wb

        for tt in range(SL // TT):
            for tb in range(NTB):
                r0 = tt * TT + tb * 128
                k.dma("sp", x1[tb][1].name, x1[tb][0][:], x_src[r0:r0 + 128, :], writes=[x1[tb][1]])
                emit_norm_transpose(k, nc, o_s[r0:r0 + 128, :], mT, tb, gc, ident, xs, hb, ss, rs, tp,
                                    [(0, 1024), (1024, 2048)])
            for n in range(4):
                wt, wb = wload(wov[:, :, n * 512:(n + 1) * 512], wc_out[1])
                for tb in range(NTB):
                    at, ab = acc[cn["acc"] % 4]
                    cn["acc"] += 1
                    for kc in range(16):
                        k.op("pe", lambda: nc.tensor.matmul(
                            at[:], lhsT=mT[0][:, kc, tb * 128:(tb + 1) * 128], rhs=wt[:, kc, :],
                            start=(kc == 0), stop=(kc == 15)), reads=[mT[1], wb], writes=[ab], inc=(kc == 15))
                    k.op("dve", lambda: nc.vector.tensor_tensor(
                        out=x1[tb][0][:, n * 512:(n + 1) * 512], in0=at[:],
                        in1=x1[tb][0][:, n * 512:(n + 1) * 512], op=ALU.add),
                        reads=[ab, x1[tb][1]], writes=[x1[tb][1]])
            for tb in range(NTB):
                emit_norm_transpose(k, nc, None, h2T, tb, gm, ident, None, hb, ss, rs, tp, [(0, D)], src_sb=x1[tb])
            for fp in range(D_FF // 512):
                wt, wb = wload(wuv[:, :, fp * 512:(fp + 1) * 512], wc_up[1])
                for fc in range(4):
                    at, ab = acc[cn["acc"] % 4]
                    cn["acc"] += 1
                    for kc in range(16):
                        k.op("pe", lambda: nc.tensor.matmul(
                            at[:], lhsT=wt[:, kc, fc * 128:(fc + 1) * 128], rhs=h2T[0][:, kc, :],
                            start=(kc == 0), stop=(kc == 15)), reads=[h2T[1], wb], writes=[ab], inc=(kc == 15))
                    tm, tmb = tmp[cn["tmp"] % 2]
                    cn["tmp"] += 1
                    k.op("act", lambda: nc.scalar.activation(out=tm[:], in_=at[:], func=AF.Relu), reads=[ab], writes=[tmb])
                    fidx = fp * 4 + fc
                    if fidx % 2 == 0:
                        k.op("dve", lambda: nc.vector.tensor_tensor(out=aT[0][:, fidx, :], in0=tm[:], in1=tm[:], op=ALU.mult),
                             reads=[tmb], writes=[aT[1]])
                    else:
                        k.op("pool", lambda: nc.gpsimd.tensor_tensor(out=aT[0][:, fidx, :], in0=tm[:], in1=tm[:], op=ALU.mult),
                             reads=[tmb], writes=[aT[1]])
            for n in range(4):
                accs = [acc[(cn["acc"] + i) % 4] for i in range(NTB)]
                cn["acc"] += NTB
                for kp in range(4):
                    wt, wb = wload(w_down[kp * 2048:(kp + 1) * 2048, n * 512:(n + 1) * 512]
                                   .rearrange("(k p) n -> p k n", p=128), wc_down[1])
                    for tb in range(NTB):
                        at, ab = accs[tb]
                        for kc in range(16):
                            last = (kp == 3 and kc == 15)
                            k.op("pe", lambda: nc.tensor.matmul(
                                at[:], lhsT=aT[0][:, kp * 16 + kc, tb * 128:(tb + 1) * 128], rhs=wt[:, kc, :],
                                start=(kp == 0 and kc == 0), stop=last),
                                reads=[aT[1], wb], writes=[ab], inc=(kc == 15))
                for tb in range(NTB):
                    at, ab = accs[tb]
                    ot, obb = ob[cn["ob"] % 3]
                    cn["ob"] += 1
                    r0 = tt * TT + tb * 128
                    k.op("dve", lambda: nc.vector.tensor_tensor(
                        out=ot[:], in0=at[:], in1=x1[tb][0][:, n * 512:(n + 1) * 512], op=ALU.add),
                        reads=[ab, x1[tb][1]], writes=[obb])
                    k.dma("sp", obb.name + "_st", x_dst[r0:r0 + 128, n * 512:(n + 1) * 512], ot[:], reads=[obb])
        _barrier(k)


def build_fused(SL=S, depth=DEPTH, phases="123"):
    from contextlib import ExitStack
    nc = bass.Bass("TRN2", target_bir_lowering=False)
    ext = lambda name, shape, dt=F32: nc.dram_tensor(name, shape, dt, kind="ExternalInput").ap()
    x = ext("x", [SL, D])
    w_in = ext("w_in", [depth, D, D_IN])
    w_out = ext("w_out", [depth, D, D])
    w_up = ext("w_up", [depth, D, D_FF])
    w_down = ext("w_down", [depth, D_FF, D])
    g_attn = ext("g_attn", [depth, D])
    g_qk = ext("g_qk", [depth, 128])
    g_cat = ext("g_cat", [depth, D])
    g_mlp = ext("g_mlp", [depth, D])
    sink_bc = ext("sink_bc", [128, depth * 16])
    biasT_d = ext("biasT", [128, 2, 16, 128])
    maskT_d = ext("maskT", [128, 2, 128])
    masks_d = ext("masks", [128, 4, 512])
    ntri_d = ext("ntri", [128, 128])
    ident_d = ext("ident", [128, 128])
    xo = nc.dram_tensor("xo", [SL, D], F32, kind="ExternalOutput").ap()
    xa = nc.dram_tensor("x_scratch", [SL, D], F32).ap()
    o_s = nc.dram_tensor("o_scratch", [SL, D], F32).ap()
    scr = {
        "qaT": nc.dram_tensor("qaT_scr", [8, 128, SL], BF16).ap(),
        "kaT2": nc.dram_tensor("kaT2_scr", [4, 128, SL], BF16).ap(),
        "va": nc.dram_tensor("va_scr", [SL, 256], BF16).ap(),
        "qbT": nc.dram_tensor("qbT_scr", [8, 128, SL], BF16).ap(),
        "kbT": nc.dram_tensor("kbT_scr", [8, 128, SL], BF16).ap(),
        "vb": nc.dram_tensor("vb_scr", [SL, 1024], BF16).ap(),
    }
    wc = {nm: (nc.dram_tensor("wc_" + nm, shp, BF16).ap(), Buf("wc_" + nm))
          for nm, shp in (("in", [D, D_IN]), ("out", [D, D]), ("up", [D, D_FF]), ("down", [D_FF, D]))}
    with ExitStack() as st0:
        k = K(nc, st0)
        ident32 = (st0.enter_context(nc.sbuf_tensor("ident32", [128, 128], F32)), Buf("ident32"))
        ident = (st0.enter_context(nc.sbuf_tensor("identb", [128, 128], BF16)), Buf("identb"))
        with nc.Block():
            k.dma("sp", "ident32", ident32[0][:], ident_d, writes=[ident32[1]])
            k.op("dve", lambda: nc.vector.tensor_copy(out=ident[0][:], in_=ident32[0][:]),
                 reads=[ident32[1]], writes=[ident[1]])
            for l in range(depth):
                x_src = x if l == 0 else xa
                x_dst = xo if l == depth - 1 else xa
                for nm, src in (("in", w_in[l]), ("out", w_out[l]), ("up", w_up[l]), ("down", w_down[l])):
                    dst, cbuf = wc[nm]
                    for c0 in range(0, src.shape[1], 512):
                        k.dma("pool", cbuf.name, dst[:, c0:c0 + 512].rearrange("(k p) n -> p k n", p=128),
                              src[:, c0:c0 + 512].rearrange("(k p) n -> p k n", p=128), writes=[cbuf])
                if "1" in phases:
                    emit_P1(k, nc, "L%dp1_" % l, SL, x_src, wc["in"], g_attn[l:l + 1, :], g_qk[l:l + 1, :], ident, scr)
                for gi in (range(4) if "2" in phases else []):
                    emit_P2(k, nc, "L%dp2g%d" % (l, gi), SL, gi, scr, biasT_d, maskT_d,
                            sink_bc, l * 16 + 4 * gi, masks_d, ntri_d, o_s,
                            parts="".join(ch for ch in phases if ch in "ab") or "ab")
                if "3" in phases:
                    emit_P3(k, nc, "L%dp3_" % l, SL, x_src, x_dst, o_s, wc["out"], wc["up"], wc["down"],
                            g_cat[l:l + 1, :], g_mlp[l:l + 1, :], ident)
            _barrier(k)
    return nc


def _t5_bucket(dist):
    max_exact = 16
    d = np.maximum(dist, 0)
    ratio = np.maximum(d, 1).astype(np.float32) / max_exact
    large = max_exact + (np.log(ratio) / math.log(128 / max_exact) * (32 - max_exact)).astype(np.int32)
    large = np.minimum(large, 31)
    return np.where(d < max_exact, d, large).astype(np.int32)


def l2_constants():
    s = np.arange(128)[:, None, None]
    j = np.arange(2)[None, :, None]
    q = np.arange(128)[None, None, :]
    dist = q + 128 - (j * 128 + s)
    valid = (dist >= 0) & (dist < 128)
    maskT = np.where(valid, 0.0, NEG).astype(np.float32)
    bucket = _t5_bucket(dist)
    sa = (np.arange(4)[None, :, None] * 128 + np.arange(128)[:, None, None])
    masks = (sa < np.arange(512)[None, None, :]).astype(np.float32)
    ntri = -(np.arange(128)[:, None] >= np.arange(128)[None, :]).astype(np.float32)
    return bucket, maskT, masks, ntri


_PROG = {}


def make_inputs(x_seq, norm_attn_g, w_in, q_norm_g, k_norm_g, sinks, rel_bias, swa_out_g, sb_out_g, w_out,
                norm_mlp_g, w_up, w_down, depth):
    f32 = np.float32
    c = lambda a: np.ascontiguousarray(np.asarray(a, f32))
    bucket, maskT, masks, ntri = l2_constants()
    biasT = c(np.asarray(rel_bias, f32)[bucket].transpose(0, 1, 3, 2))
    sink_bc = c(np.broadcast_to(np.asarray(sinks, f32)[:depth].reshape(1, depth * 16), (128, depth * 16)))
    shared = {
        "w_in": c(w_in[:depth]), "w_out": c(w_out[:depth]), "w_up": c(w_up[:depth]), "w_down": c(w_down[:depth]),
        "g_attn": c(norm_attn_g[:depth]),
        "g_qk": c(np.concatenate([np.asarray(q_norm_g, f32)[:depth], np.asarray(k_norm_g, f32)[:depth]], 1)),
        "g_cat": c(np.concatenate([np.asarray(swa_out_g, f32)[:depth], np.asarray(sb_out_g, f32)[:depth]], 1)),
        "g_mlp": c(norm_mlp_g[:depth]),
        "sink_bc": sink_bc, "biasT": biasT, "maskT": maskT, "masks": masks, "ntri": ntri,
        "ident": np.eye(128, dtype=f32)}
    return [dict(shared, x=c(xs)) for xs in x_seq]


def kernel(x, norm_attn_g, w_in, q_norm_g, k_norm_g, sinks, rel_bias, swa_out_g, sb_out_g, w_out,
           norm_mlp_g, w_up, w_down):
    x = np.asarray(x, np.float32)
    nb, sl = x.shape[0], x.shape[1]
    key = (sl, DEPTH)
    if key not in _PROG:
        _PROG[key] = build_fused(sl, DEPTH)
    in_maps = make_inputs([x[b] for b in range(nb)], norm_attn_g, w_in, q_norm_g, k_norm_g, sinks, rel_bias,
                          swa_out_g, sb_out_g, w_out, norm_mlp_g, w_up, w_down, DEPTH)
    res = run_bass_kernel_spmd(_PROG[key], in_maps, core_ids=list(range(nb)))
    return np.stack([res.results[b]["xo"] for b in range(nb)], 0).astype(np.float32)
```

```python
import math
import numpy as np
import ml_dtypes
import concourse.bass as bass
import concourse.mybir as mybir
from concourse.bass_utils import run_bass_kernel_spmd

F32 = mybir.dt.float32
BF16 = mybir.dt.bfloat16
AF = mybir.ActivationFunctionType
ALU = mybir.AluOpType
AX = mybir.AxisListType

D = 2048
B = 2
S = 8192
DEPTH = 4
DH = 64
NQA = 16
NKA = 4
NSB = 16
D_IN = 4608
D_FF = 8192
EPS = 1e-6
NCORES = 8
TOK = (B * S) // NCORES
TT = 512
NEG = -30000.0


class Buf:
    __slots__ = ("name", "writer", "readers")

    def __init__(self, name):
        self.name = name
        self.writer = None
        self.readers = {}


class K:
    def __init__(self, nc, stack):
        self.nc = nc
        self.eng = {"pe": nc.tensor, "act": nc.scalar, "dve": nc.vector, "pool": nc.gpsimd, "sp": nc.sync}
        self.sem = {}
        self.cnt = {}
        for e in ("pe", "act", "dve", "pool"):
            self.sem[e] = stack.enter_context(nc.semaphore("s_" + e))
            self.cnt[e] = 0
        self.seen = {e: {} for e in self.eng}
        self.stack = stack
        self.dma_sems = {}
        self.dma_cnt = {}
        self.semobj = {e: self.sem[e] for e in self.sem}
        self.out_tokens = []

    def dma_sem(self, key):
        if key not in self.dma_sems:
            self.dma_sems[key] = self.stack.enter_context(self.nc.semaphore("d_" + key))
            self.dma_cnt[key] = 0
            self.semobj["d_" + key] = self.dma_sems[key]
        return self.dma_sems[key]

    def _wait(self, e, tok):
        if tok is None:
            return
        key, val = tok
        if key == e and e == "pe":
            return
        if self.seen[e].get(key, 0) >= val:
            return
        self.eng[e].wait_ge(self.semobj[key], val)
        self.seen[e][key] = val

    def _deps(self, e, reads, writes):
        for b in reads:
            self._wait(e, b.writer)
        for b in writes:
            self._wait(e, b.writer)
            for t in list(b.readers.items()):
                if t[0] != e:
                    self._wait(e, t)

    def _mark(self, tok, reads, writes):
        for b in reads:
            if b.readers.get(tok[0], 0) < tok[1]:
                b.readers[tok[0]] = tok[1]
        for b in writes:
            b.writer = tok
            b.readers = {}

    def op(self, e, fn, reads=(), writes=(), inc=True):
        self._deps(e, reads, writes)
        ins = fn()
        if inc:
            self.cnt[e] += 1
            ins.then_inc(self.sem[e], 1)
            tok = (e, self.cnt[e])
        else:
            tok = (e, self.cnt[e] + 1)
        self._mark(tok, reads, writes)
        return tok

    def dma(self, q, key, out, in_, reads=(), writes=()):
        sem = self.dma_sem(key)
        self._deps(q, reads, writes)
        self.eng[q].dma_start(out=out, in_=in_).then_inc(sem, 16)
        self.dma_cnt[key] += 16
        tok = ("d_" + key, self.dma_cnt[key])
        self._mark(tok, reads, writes)
        return tok

    def finish(self, e="sp"):
        last = {}
        for t in self.out_tokens:
            last[t[0]] = max(last.get(t[0], 0), t[1])
        for t in last.items():
            self._wait(e, t)
        for x in ("pe", "act", "dve", "pool"):
            if self.cnt[x]:
                self._wait(e, (x, self.cnt[x]))


def _bc(ap, shape):
    return ap.to_broadcast(list(shape))


def emit_norm_transpose(k, nc, src_dram_rows, hT, tb, g_bc, ident, xs, hb, ss, rs, tp, halves, junk=None,
                        keep=None, src_sb=None):
    if src_sb is not None:
        xt, xb = src_sb
    else:
        xt, xb = xs
        k.dma("sp", xb.name, xt[:], src_dram_rows, writes=[xb])
    if junk is None:
        junk = hb
    k.op("dve", lambda: nc.vector.memset(ss[0][:], 0.0), writes=[ss[1]])
    for i, (lo, hi) in enumerate(halves):
        k.op("act", lambda lo=lo, hi=hi, i=i: nc.scalar.activation(
            out=junk[0][:, 0:hi - lo], in_=xt[:, lo:hi], func=AF.Square, accum_out=ss[0][:, i:i + 1]),
            reads=[xb], writes=[junk[1], ss[1]])
    nh = len(halves)
    w = halves[0][1] - halves[0][0]
    k.op("dve", lambda: nc.vector.tensor_scalar(out=rs[0][:, 0:nh], in0=ss[0][:, 0:nh], scalar1=1.0 / w,
                                                scalar2=EPS, op0=ALU.mult, op1=ALU.add),
         reads=[ss[1]], writes=[rs[1]])
    k.op("act", lambda: nc.scalar.activation(out=rs[0][:, 0:nh], in_=rs[0][:, 0:nh], func=AF.Sqrt),
         reads=[rs[1]], writes=[rs[1]])
    k.op("dve", lambda: nc.vector.reciprocal(out=rs[0][:, 0:nh], in_=rs[0][:, 0:nh]),
         reads=[rs[1]], writes=[rs[1]])
    for i, (lo, hi) in enumerate(halves):
        k.op("dve", lambda lo=lo, hi=hi, i=i: nc.vector.scalar_tensor_tensor(
            out=hb[0][:, lo:hi], in0=xt[:, lo:hi], scalar=rs[0][:, i:i + 1], in1=g_bc[0][:, lo:hi],
            op0=ALU.mult, op1=ALU.mult), reads=[xb, rs[1], g_bc[1]], writes=[hb[1]])
    if keep is not None:
        k.op("pool", lambda: nc.gpsimd.tensor_copy(out=keep[0][:], in_=xt[:]), reads=[xb], writes=[keep[1]])
    for cg in range(4):
        tpt, tpb = tp[cg % len(tp)]
        for j in range(4):
            c = cg * 4 + j
            k.op("pe", lambda c=c, j=j, tpt=tpt: nc.tensor.transpose(
                out=tpt[:, j, :], in_=hb[0][:, c * 128:(c + 1) * 128], identity=ident[0][:]),
                reads=[hb[1], ident[1]], writes=[tpb], inc=(j == 3))
        eng = "act" if cg % 2 == 0 else "dve"
        if eng == "act":
            k.op("act", lambda cg=cg, tpt=tpt: nc.scalar.copy(
                out=hT[0][:, cg * 4:cg * 4 + 4, tb * 128:(tb + 1) * 128], in_=tpt[:]),
                reads=[tpb], writes=[hT[1]])
        else:
            k.op("dve", lambda cg=cg, tpt=tpt: nc.vector.tensor_copy(
                out=hT[0][:, cg * 4:cg * 4 + 4, tb * 128:(tb + 1) * 128], in_=tpt[:]),
                reads=[tpb], writes=[hT[1]])


def load_bc(k, nc, tile_buf, dram_row_ap, key):
    t, b = tile_buf
    k.dma("sp", key, t[:], dram_row_ap.partition_broadcast(128), writes=[b])


def _barrier(k):
    toks = [(e, k.cnt[e]) for e in ("pe", "act", "dve", "pool") if k.cnt[e]]
    toks += [("d_" + key, c) for key, c in k.dma_cnt.items() if c]
    for e in ("pe", "act", "dve", "pool", "sp"):
        for t in toks:
            k._wait(e, t)


class Scope:
    def __init__(self, nc, st, pfx):
        self.nc, self.st, self.pfx = nc, st, pfx

    def T(self, name, shape, dt):
        return (self.st.enter_context(self.nc.sbuf_tensor(self.pfx + name, shape, dt)), Buf(name))

    def P(self, name, shape, dt):
        return (self.st.enter_context(self.nc.psum_tensor(self.pfx + name, shape, dt)), Buf(name))


def emit_P1(k, nc, pfx, SL, x_src, wc_in, g_attn_row, g_qk_row, ident, scr):
    from contextlib import ExitStack
    wv = wc_in[0].rearrange("(k p) n -> p k n", p=128)
    scale = 1.0 / math.sqrt(DH)
    qaT_v = scr["qaT"].rearrange("h p s -> p h s")
    kaT_v = scr["kaT2"].rearrange("h p s -> p h s")
    with ExitStack() as st:
        sc = Scope(nc, st, pfx)
        T, P = sc.T, sc.P
        g_bc = T("g_bc", [128, D], F32)
        gq = T("gq", [128, 128], F32)
        xs = [T("xs%d" % i, [128, D], F32) for i in range(2)]
        hb = [T("hb%d" % i, [128, D], BF16) for i in range(2)]
        ss = [T("ss%d" % i, [128, 2], F32) for i in range(2)]
        rs = [T("rs%d" % i, [128, 2], F32) for i in range(2)]
        hT = [T("hT%d" % i, [128, 16, TT], BF16) for i in range(2)]
        wr = [T("w%d" % i, [128, 16, 512], BF16) for i in range(2)]
        sq = [T("sq%d" % i, [128, 512], F32) for i in range(2)]
        qs = [T("qs%d" % i, [128, 8], F32) for i in range(2)]
        qn = [T("qn%d" % i, [128, 512], F32) for i in range(2)]
        ob = [T("ob%d" % i, [128, 512], BF16) for i in range(3)]
        kk = [T("kk%d" % i, [128, 4, 2, 64], BF16) for i in range(2)]
        qst = [T("qst%d" % i, [128, 4, 128], BF16) for i in range(2)]
        tp = [P("tp%d" % i, [128, 4, 128], BF16) for i in range(2)]
        acc = [P("acc%d" % i, [128, 512], F32) for i in range(4)]
        load_bc(k, nc, g_bc, g_attn_row, "g_bc")
        load_bc(k, nc, gq, g_qk_row, "gq")
        cn = {"w": 0, "ob": 0, "acc": 0, "tp": 0, "qst": 0, "kk": 0}

        def nxt(ring, key):
            r = ring[cn[key] % len(ring)]
            cn[key] += 1
            return r

        def transposes_out(src_fn, src_buf, dst_ap):
            tpt, tpb = nxt(tp, "tp")
            for j in range(4):
                k.op("pe", lambda j=j: nc.tensor.transpose(out=tpt[:, j, :], in_=src_fn(j), identity=ident[0][:]),
                     reads=[src_buf, ident[1]], writes=[tpb], inc=(j == 3))
            qt_, qtb = nxt(qst, "qst")
            k.op("dve", lambda: nc.vector.tensor_copy(out=qt_[:], in_=tpt[:]), reads=[tpb], writes=[qtb])
            k.dma("sp", qtb.name + "_st", dst_ap, qt_[:], reads=[qtb])

        for tt in range(SL // TT):
            hTt = hT[tt % 2]
            t0 = tt * TT
            for tb in range(TT // 128):
                i = (tt * 4 + tb) % 2
                r0 = t0 + tb * 128
                emit_norm_transpose(k, nc, x_src[r0:r0 + 128, :], hTt, tb, g_bc, ident, xs[i], hb[i],
                                    ss[i], rs[i], tp, [(0, D)])
            for pc in range(9):
                wt, wb = nxt(wr, "w")
                k.dma("sp", wb.name, wt[:], wv[:, :, pc * 512:(pc + 1) * 512], reads=[wc_in[1]], writes=[wb])
                if pc in (3, 4, 5, 6):
                    dst = scr["qbT"] if pc < 5 else scr["kbT"]
                    for fc in range(4):
                        at, ab = nxt(acc, "acc")
                        for kc in range(16):
                            k.op("pe", lambda kc=kc, fc=fc: nc.tensor.matmul(
                                at[:], lhsT=wt[:, kc, fc * 128:(fc + 1) * 128], rhs=hTt[0][:, kc, :],
                                start=(kc == 0), stop=(kc == 15)),
                                reads=[hTt[1], wb], writes=[ab], inc=(kc == 15))
                        ot, obb = nxt(ob, "ob")
                        if pc < 5:
                            k.op("act", lambda: nc.scalar.mul(ot[:], at[:], scale), reads=[ab], writes=[obb])
                        elif fc % 2 == 0:
                            k.op("act", lambda: nc.scalar.copy(out=ot[:], in_=at[:]), reads=[ab], writes=[obb])
                        else:
                            k.op("dve", lambda: nc.vector.tensor_copy(out=ot[:], in_=at[:]), reads=[ab], writes=[obb])
                        hpi = ((pc - 3) % 2) * 4 + fc
                        k.dma("sp", obb.name + "_st", dst[hpi, :, t0:t0 + TT], ot[:], reads=[obb])
                    continue
                for tb in range(TT // 128):
                    at, ab = nxt(acc, "acc")
                    for kc in range(16):
                        k.op("pe", lambda kc=kc, tb=tb: nc.tensor.matmul(
                            at[:], lhsT=hTt[0][:, kc, tb * 128:(tb + 1) * 128], rhs=wt[:, kc, :],
                            start=(kc == 0), stop=(kc == 15)),
                            reads=[hTt[1], wb], writes=[ab], inc=(kc == 15))
                    ot, obb = nxt(ob, "ob")
                    r0 = t0 + tb * 128
                    if pc in (0, 1, 2):
                        j = (tt * 36 + pc * 4 + tb) % 2
                        nq = 512 if pc < 2 else 256
                        nh = nq // DH
                        goff = 0 if pc < 2 else 64
                        sc_ = scale if pc < 2 else 1.0
                        k.op("act", lambda: nc.scalar.activation(out=sq[j][0][:, 0:nq], in_=at[:, 0:nq], func=AF.Square),
                             reads=[ab], writes=[sq[j][1]])
                        k.op("dve", lambda: nc.vector.tensor_reduce(
                            out=qs[j][0][:, 0:nh], in_=sq[j][0][:, 0:nq].rearrange("p (h d) -> p h d", d=DH),
                            axis=AX.X, op=ALU.add), reads=[sq[j][1]], writes=[qs[j][1]])
                        k.op("dve", lambda: nc.vector.tensor_scalar(
                            out=qs[j][0][:, 0:nh], in0=qs[j][0][:, 0:nh], scalar1=1.0 / DH, scalar2=EPS,
                            op0=ALU.mult, op1=ALU.add), reads=[qs[j][1]], writes=[qs[j][1]])
                        k.op("act", lambda: nc.scalar.activation(out=qs[j][0][:, 0:nh], in_=qs[j][0][:, 0:nh], func=AF.Sqrt),
                             reads=[qs[j][1]], writes=[qs[j][1]])
                        k.op("dve", lambda: nc.vector.reciprocal(out=qs[j][0][:, 0:nh], in_=qs[j][0][:, 0:nh]),
                             reads=[qs[j][1]], writes=[qs[j][1]])
                        k.op("dve", lambda: nc.vector.tensor_tensor(
                            out=qn[j][0][:, 0:nq].rearrange("p (h d) -> p h d", d=DH),
                            in0=at[:, 0:nq].rearrange("p (h d) -> p h d", d=DH),
                            in1=_bc(qs[j][0][:, 0:nh].unsqueeze(2), [128, nh, DH]), op=ALU.mult),
                            reads=[ab, qs[j][1]], writes=[qn[j][1]])
                        k.op("dve", lambda: nc.vector.scalar_tensor_tensor(
                            out=ot[:, 0:nq].rearrange("p (h d) -> p h d", d=DH),
                            in0=qn[j][0][:, 0:nq].rearrange("p (h d) -> p h d", d=DH), scalar=sc_,
                            in1=_bc(gq[0][:, goff:goff + DH].unsqueeze(1), [128, nh, DH]),
                            op0=ALU.mult, op1=ALU.mult), reads=[qn[j][1], gq[1]], writes=[obb])
                        if pc < 2:
                            transposes_out(lambda jj: ot[:, jj * 128:(jj + 1) * 128], obb,
                                           qaT_v[:, pc * 4:pc * 4 + 4, r0:r0 + 128])
                        else:
                            k.op("act", lambda: nc.scalar.copy(out=ot[:, 256:512], in_=at[:, 256:512]),
                                 reads=[ab], writes=[obb])
                            k.dma("sp", obb.name + "_st", scr["va"][r0:r0 + 128, :], ot[:, 256:512], reads=[obb])
                            kt, kkb = nxt(kk, "kk")
                            k.op("dve", lambda: nc.vector.tensor_copy(
                                out=kt[:, :, 0, :], in_=ot[:, 0:256].rearrange("p (g d) -> p g d", d=DH)),
                                reads=[obb], writes=[kkb])
                            k.op("pool", lambda: nc.gpsimd.tensor_copy(
                                out=kt[:, :, 1, :], in_=ot[:, 0:256].rearrange("p (g d) -> p g d", d=DH)),
                                reads=[obb], writes=[kkb])
                            transposes_out(lambda jj: kt[:, jj].rearrange("p a d -> p (a d)"), kkb,
                                           kaT_v[:, 0:4, r0:r0 + 128])
                    else:
                        if tb % 2 == 0:
                            k.op("act", lambda: nc.scalar.copy(out=ot[:], in_=at[:]), reads=[ab], writes=[obb])
                        else:
                            k.op("dve", lambda: nc.vector.tensor_copy(out=ot[:], in_=at[:]), reads=[ab], writes=[obb])
                        k.dma("sp", obb.name + "_st", scr["vb"][r0:r0 + 128, (pc - 7) * 512:(pc - 6) * 512], ot[:],
                              reads=[obb])
        _barrier(k)


def emit_P2(k, nc, pfx, SL, gi, scr, biasT_d, maskT_d, sink_all, sink_c0, masks_d, ntri_d, o_s, parts="ab"):
    from contextlib import ExitStack
    NKB = SL // 128
    NQT = SL // 512
    def _swa():
        with ExitStack() as st:
            sc = Scope(nc, st, pfx + "a_")
            T, P = sc.T, sc.P
            qaT = T("qaT_s", [128, 2, SL], BF16)
            kaT = T("kaT_s", [128, SL], BF16)
            va1 = T("va1", [128, NKB, 72], BF16)
            biasT = T("biasT_s", [128, 2, 16, 128], F32)
            maskT = T("maskT_s", [128, 2, 128], F32)
            sinkt = T("sinkt", [128, sink_all.shape[1]], F32)
            esink = T("esink", [128, sink_all.shape[1]], F32)
            stt = [T("stt%d" % i, [128, 4, 128], F32) for i in range(2)]
            pT = [T("pT%d" % i, [128, 4, 128], BF16) for i in range(4)]
            den = [T("den%d" % i, [128, 4], F32) for i in range(2)]
            oa = [T("oa%d" % i, [128, 4, 64], F32) for i in range(2)]
            sps = [[P("sps%d_%d" % (i, par), [128, 2, 128], F32) for par in range(2)] for i in range(2)]
            ops = [P("ops%d" % i, [128, 4, 65], F32) for i in range(2)]
            k.dma("sp", "qaT_s", qaT[0][:], scr["qaT"][2 * gi:2 * gi + 2].rearrange("h p s -> p h s"), writes=[qaT[1]])
            k.dma("sp", "kaT_s", kaT[0][:], scr["kaT2"][gi], writes=[kaT[1]])
            k.op("dve", lambda: nc.vector.memset(va1[0][:], 1.0), writes=[va1[1]])
            for c0 in range(0, NKB, 8):
                k.dma("sp", "va1", va1[0][:, c0:c0 + 8, 0:64],
                      scr["va"][c0 * 128:(c0 + 8) * 128, gi * 64:(gi + 1) * 64]
                      .rearrange("(kb p) d -> p kb d", p=128), writes=[va1[1]])
            k.dma("sp", "biasT_s", biasT[0][:], biasT_d, writes=[biasT[1]])
            k.dma("sp", "maskT_s", maskT[0][:], maskT_d, writes=[maskT[1]])
            k.dma("sp", "sinkt", sinkt[0][:], sink_all, writes=[sinkt[1]])
            for j in range(2):
                k.op("dve", lambda: nc.vector.tensor_tensor(
                    out=biasT[0][:, j], in0=biasT[0][:, j],
                    in1=_bc(maskT[0][:, j].unsqueeze(1), [128, 16, 128]), op=ALU.add),
                    reads=[biasT[1], maskT[1]], writes=[biasT[1]])
            k.op("act", lambda: nc.scalar.activation(out=esink[0][:], in_=sinkt[0][:], func=AF.Exp),
                 reads=[sinkt[1]], writes=[esink[1]])
            nst = 0
            npt = 0
            for qb in range(NKB):
                js = [1] if qb == 0 else [0, 1]
                pts = []
                for j in js:
                    kb = qb - 1 + j
                    sp2 = sps[nst % 2]
                    stt_t, stt_b = stt[nst % 2]
                    nst += 1
                    for h in range(4):
                        hp, par = h // 2, h % 2
                        pb = 64 * par
                        spt, spb = sp2[par]
                        k.op("pe", lambda: nc.tensor.matmul(
                            spt[:, hp, :], lhsT=kaT[0][pb:pb + 64, kb * 128:(kb + 1) * 128],
                            rhs=qaT[0][pb:pb + 64, hp, qb * 128:(qb + 1) * 128], start=True, stop=True),
                            reads=[kaT[1], qaT[1]], writes=[spb], inc=(h >= 2))
                    for par in range(2):
                        spt, spb = sp2[par]
                        k.op("dve", lambda: nc.vector.tensor_tensor(
                            out=stt_t[:].rearrange("p (a b) q -> p a b q", b=2)[:, :, par, :], in0=spt[:],
                            in1=biasT[0][:, j, 4 * gi:4 * gi + 4, :].rearrange("p (a b) q -> p a b q", b=2)[:, :, par, :],
                            op=ALU.add), reads=[spb, biasT[1]], writes=[stt_b])
                    ptt, ptb = pT[npt % 4]
                    npt += 1
                    k.op("act", lambda: nc.scalar.activation(out=ptt[:], in_=stt_t[:], func=AF.Exp),
                         reads=[stt_b], writes=[ptb])
                    pts.append((ptt, ptb, kb))
                opt, opb = ops[qb % 2]
                for h in range(4):
                    for idx, (ptt, ptb, kb) in enumerate(pts):
                        k.op("pe", lambda: nc.tensor.matmul(
                            opt[:, h, :], lhsT=ptt[:, h, :], rhs=va1[0][:, kb, 0:65],
                            start=(idx == 0), stop=(idx == len(pts) - 1)),
                            reads=[ptb, va1[1]], writes=[opb], inc=(h == 3 and idx == len(pts) - 1))
                dn, dnb = den[qb % 2]
                oat, oab = oa[qb % 2]
                k.op("dve", lambda: nc.vector.tensor_tensor(out=dn[:], in0=opt[:, :, 64], in1=esink[0][:, sink_c0:sink_c0 + 4], op=ALU.add),
                     reads=[opb, esink[1]], writes=[dnb])
                k.op("dve", lambda: nc.vector.reciprocal(out=dn[:], in_=dn[:]), reads=[dnb], writes=[dnb])
                k.op("dve", lambda: nc.vector.tensor_tensor(
                    out=oat[:], in0=opt[:, :, 0:64], in1=_bc(dn[:].unsqueeze(2), [128, 4, 64]), op=ALU.mult),
                    reads=[opb, dnb], writes=[oab])
                k.dma("sp", oab.name + "_st", o_s[qb * 128:(qb + 1) * 128, gi * 256:(gi + 1) * 256],
                      oat[:].rearrange("p h d -> p (h d)"), reads=[oab])
            _barrier(k)

    def _sb():
        with ExitStack() as st:
            sc = Scope(nc, st, pfx + "b_")
            T, P = sc.T, sc.P
            qbT = T("qbT_s", [128, 2, SL], BF16)
            kbT = T("kbT_s", [128, 2, SL], BF16)
            vb = T("vb_s", [128, NKB, 256], BF16)
            m32 = T("m32", [128, 4, 512], F32)
            masks = T("masks_s", [128, 4, 512], BF16)
            t32 = T("t32", [128, 128], F32)
            ntri = T("ntri_s", [128, 128], BF16)
            nones = T("nones_s", [128, 128], BF16)
            e32 = [T("e32_%d" % i, [128, 512], F32) for i in range(2)]
            spr = [T("sp_%d" % i, [128, 512], BF16) for i in range(3)]
            wr_ = [T("wv_%d" % i, [128, 512], BF16) for i in range(2)]
            R32 = T("R32", [128, 512], F32)
            Rb = [T("Rb_%d" % i, [128, 512], BF16) for i in range(2)]
            ost = [T("ost_%d" % i, [128, 4, 64], F32) for i in range(2)]
            zps = [P("zps%d" % i, [128, 512], F32) for i in range(2)]
            eps_ = [P("eps%d" % i, [128, 512], F32) for i in range(2)]
            oacc = [P("oacc%d" % i, [128, 4, 64], F32) for i in range(2)]
            k.dma("sp", "qbT_s", qbT[0][:], scr["qbT"][2 * gi:2 * gi + 2].rearrange("h p s -> p h s"), writes=[qbT[1]])
            k.dma("sp", "kbT_s", kbT[0][:], scr["kbT"][2 * gi:2 * gi + 2].rearrange("h p s -> p h s"), writes=[kbT[1]])
            for c0 in range(0, NKB, 8):
                k.dma("sp", "vb_s", vb[0][:, c0:c0 + 8, :],
                      scr["vb"][c0 * 128:(c0 + 8) * 128, gi * 256:(gi + 1) * 256]
                      .rearrange("(kb p) f -> p kb f", p=128), writes=[vb[1]])
            k.dma("sp", "m32", m32[0][:], masks_d, writes=[m32[1]])
            k.dma("sp", "t32", t32[0][:], ntri_d, writes=[t32[1]])
            k.op("dve", lambda: nc.vector.tensor_copy(out=masks[0][:], in_=m32[0][:]), reads=[m32[1]], writes=[masks[1]])
            k.op("dve", lambda: nc.vector.tensor_copy(out=ntri[0][:], in_=t32[0][:]), reads=[t32[1]], writes=[ntri[1]])
            k.op("dve", lambda: nc.vector.memset(nones[0][:], -1.0), writes=[nones[1]])
            RENG, RE = "dve", nc.vector
            units = [(h, qt, kb) for h in range(4) for qt in range(NQT) for kb in range(4 * qt + 3, -1, -1)]
            state = {}

            def stageA(u, n):
                h, qt, kb = u
                hp, pb = h // 2, 64 * (h % 2)
                zt, zb = zps[n % 2]
                et, eb = e32[n % 2]
                st_, sbb = spr[n % 3]
                k.op("pe", lambda: nc.tensor.matmul(
                    zt[:], lhsT=kbT[0][pb:pb + 64, hp, kb * 128:(kb + 1) * 128],
                    rhs=qbT[0][pb:pb + 64, hp, qt * 512:(qt + 1) * 512], start=True, stop=True),
                    reads=[kbT[1], qbT[1]], writes=[zb])
                k.op("act", lambda: nc.scalar.activation(out=et[:], in_=zt[:], func=AF.Exp), reads=[zb], writes=[eb])
                k.op("act", lambda: nc.scalar.activation(out=st_[:], in_=et[:], func=AF.Ln, bias=1.0),
                     reads=[eb], writes=[sbb])
                i = kb - 4 * qt
                if i >= 0:
                    k.op("pool", lambda: nc.gpsimd.tensor_tensor(out=st_[:], in0=st_[:], in1=masks[0][:, i, :], op=ALU.mult),
                         reads=[sbb, masks[1]], writes=[sbb])
                state[n] = (st_, sbb)

            def stageB(u, n):
                h, qt, kb = u
                hp, pb = h // 2, 64 * (h % 2)
                st_, sbb = state.pop(n)
                first = (kb == 4 * qt + 3)
                et, eb = eps_[n % 2]
                wt, wb = wr_[n % 2]
                ot, obuf = oacc[(h * NQT + qt) % 2]
                rbt, rbb = Rb[n % 2]
                k.op("pe", lambda: nc.tensor.matmul(
                    et[:], lhsT=kbT[0][pb:pb + 64, hp, kb * 128:(kb + 1) * 128],
                    rhs=qbT[0][pb:pb + 64, hp, qt * 512:(qt + 1) * 512], start=True, stop=False),
                    reads=[kbT[1], qbT[1]], writes=[eb], inc=False)
                k.op("pe", lambda: nc.tensor.matmul(et[:], lhsT=ntri[0][:], rhs=st_[:], start=False, stop=first),
                     reads=[ntri[1], sbb], writes=[eb], inc=first)
                if not first:
                    k.op("pe", lambda: nc.tensor.matmul(et[:], lhsT=nones[0][:], rhs=rbt[:], start=False, stop=True),
                         reads=[nones[1], rbb], writes=[eb])
                k.op("act", lambda: nc.scalar.activation(out=wt[:], in_=et[:], func=AF.Exp), reads=[eb], writes=[wb])
                i = kb - 4 * qt
                if i >= 0:
                    k.op("dve", lambda: nc.vector.tensor_tensor(out=wt[:], in0=wt[:], in1=masks[0][:, i, :], op=ALU.mult),
                         reads=[wb, masks[1]], writes=[wb])
                for sub in range(4):
                    k.op("pe", lambda: nc.tensor.matmul(
                        ot[:, sub, :], lhsT=wt[:, sub * 128:(sub + 1) * 128], rhs=vb[0][:, kb, h * 64:(h + 1) * 64],
                        start=first, stop=(kb == 0)), reads=[vb[1], wb], writes=[obuf], inc=(sub == 3))
                if kb > 0:
                    nrb, nrbb = Rb[(n + 1) % 2]
                    if first:
                        k.op(RENG, lambda: RE.tensor_copy(out=R32[0][:], in_=st_[:]), reads=[sbb], writes=[R32[1]])
                    else:
                        k.op(RENG, lambda: RE.tensor_tensor(out=R32[0][:], in0=R32[0][:], in1=st_[:], op=ALU.add),
                             reads=[sbb, R32[1]], writes=[R32[1]])
                    k.op(RENG, lambda: RE.tensor_copy(out=nrb[:], in_=R32[0][:]), reads=[R32[1]], writes=[nrbb])
                else:
                    o_t, o_b = ost[(h * NQT + qt) % 2]
                    k.op("dve", lambda: nc.vector.tensor_copy(out=o_t[:], in_=ot[:]), reads=[obuf], writes=[o_b])
                    c0 = 1024 + (4 * gi + h) * 64
                    k.dma("sp", o_b.name + "_st",
                          o_s[qt * 512:(qt + 1) * 512, c0:c0 + 64].rearrange("(sub p) d -> p sub d", p=128),
                          o_t[:], reads=[o_b])

            stageA(units[0], 0)
            for n, u in enumerate(units):
                if n + 1 < len(units):
                    stageA(units[n + 1], n + 1)
                stageB(u, n)
            _barrier(k)

    if "a" in parts:
        _swa()
    if "b" in parts:
        _sb()


def emit_P3(k, nc, pfx, SL, x_src, x_dst, o_s, wc_out, wc_up, wc_down, g_cat_row, g_mlp_row, ident):
    from contextlib import ExitStack
    wov = wc_out[0].rearrange("(k p) n -> p k n", p=128)
    wuv = wc_up[0].rearrange("(k p) n -> p k n", p=128)
    w_down = wc_down[0]
    NTB = TT // 128
    with ExitStack() as st:
        sc = Scope(nc, st, pfx)
        T, P = sc.T, sc.P
        gc = T("gc", [128, D], F32)
        gm = T("gm", [128, D], F32)
        xs = T("xs", [128, D], F32)
        hb = T("hb", [128, D], BF16)
        ss = T("ss", [128, 2], F32)
        rs = T("rs", [128, 2], F32)
        mT = T("mT", [128, 16, TT], BF16)
        h2T = T("h2T", [128, 16, TT], BF16)
        x1 = [T("x1_%d" % i, [128, D], F32) for i in range(NTB)]
        aT = T("aT", [128, 64, TT], BF16)
        wr = [T("w%d" % i, [128, 16, 512], BF16) for i in range(2)]
        tmp = [T("tmp%d" % i, [128, 512], F32) for i in range(2)]
        ob = [T("ob32_%d" % i, [128, 512], F32) for i in range(3)]
        tp = [P("tp%d" % i, [128, 4, 128], BF16) for i in range(2)]
        acc = [P("acc%d" % i, [128, 512], F32) for i in range(4)]
        load_bc(k, nc, gc, g_cat_row, "gc")
        load_bc(k, nc, gm, g_mlp_row, "gm")
        cn = {"w": 0, "ob": 0, "acc": 0, "tmp": 0}

        def wload(src, cbuf):
            wt, wb = wr[cn["w"] % 2]
            cn["w"] += 1
            k.dma("sp", wb.name, wt[:], src, reads=[cbuf], writes=[wb])
            return wt, wb

        for tt in range(SL // TT):
            for tb in range(NTB):
                r0 = tt * TT + tb * 128
                k.dma("sp", x1[tb][1].name, x1[tb][0][:], x_src[r0:r0 + 128, :], writes=[x1[tb][1]])
                emit_norm_transpose(k, nc, o_s[r0:r0 + 128, :], mT, tb, gc, ident, xs, hb, ss, rs, tp,
                                    [(0, 1024), (1024, 2048)])
            for n in range(4):
                wt, wb = wload(wov[:, :, n * 512:(n + 1) * 512], wc_out[1])
                for tb in range(NTB):
                    at, ab = acc[cn["acc"] % 4]
                    cn["acc"] += 1
                    for kc in range(16):
                        k.op("pe", lambda: nc.tensor.matmul(
                            at[:], lhsT=mT[0][:, kc, tb * 128:(tb + 1) * 128], rhs=wt[:, kc, :],
                            start=(kc == 0), stop=(kc == 15)), reads=[mT[1], wb], writes=[ab], inc=(kc == 15))
                    k.op("dve", lambda: nc.vector.tensor_tensor(
                        out=x1[tb][0][:, n * 512:(n + 1) * 512], in0=at[:],
                        in1=x1[tb][0][:, n * 512:(n + 1) * 512], op=ALU.add),
                        reads=[ab, x1[tb][1]], writes=[x1[tb][1]])
            for tb in range(NTB):
                emit_norm_transpose(k, nc, None, h2T, tb, gm, ident, None, hb, ss, rs, tp, [(0, D)], src_sb=x1[tb])
            for fp in range(D_FF // 512):
                wt, wb = wload(wuv[:, :, fp * 512:(fp + 1) * 512], wc_up[1])
                for fc in range(4):
                    at, ab = acc[cn["acc"] % 4]
                    cn["acc"] += 1
                    for kc in range(16):
                        k.op("pe", lambda: nc.tensor.matmul(
                            at[:], lhsT=wt[:, kc, fc * 128:(fc + 1) * 128], rhs=h2T[0][:, kc, :],
                            start=(kc == 0), stop=(kc == 15)), reads=[h2T[1], wb], writes=[ab], inc=(kc == 15))
                    tm, tmb = tmp[cn["tmp"] % 2]
                    cn["tmp"] += 1
                    k.op("act", lambda: nc.scalar.activation(out=tm[:], in_=at[:], func=AF.Relu), reads=[ab], writes=[tmb])
                    fidx = fp * 4 + fc
                    if fidx % 2 == 0:
                        k.op("dve", lambda: nc.vector.tensor_tensor(out=aT[0][:, fidx, :], in0=tm[:], in1=tm[:], op=ALU.mult),
                             reads=[tmb], writes=[aT[1]])
                    else:
                        k.op("pool", lambda: nc.gpsimd.tensor_tensor(out=aT[0][:, fidx, :], in0=tm[:], in1=tm[:], op=ALU.mult),
                             reads=[tmb], writes=[aT[1]])
            for n in range(4):
                accs = [acc[(cn["acc"] + i) % 4] for i in range(NTB)]
                cn["acc"] += NTB
                for kp in range(4):
                    wt, wb = wload(w_down[kp * 2048:(kp + 1) * 2048, n * 512:(n + 1) * 512]
                                   .rearrange("(k p) n -> p k n", p=128), wc_down[1])
                    for tb in range(NTB):
                        at, ab = accs[tb]
                        for kc in range(16):
                            last = (kp == 3 and kc == 15)
                            k.op("pe", lambda: nc.tensor.matmul(
                                at[:], lhsT=aT[0][:, kp * 16 + kc, tb * 128:(tb + 1) * 128], rhs=wt[:, kc, :],
                                start=(kp == 0 and kc == 0), stop=last),
                                reads=[aT[1], wb], writes=[ab], inc=(kc == 15))
                for tb in range(NTB):
                    at, ab = accs[tb]
                    ot, obb = ob[cn["ob"] % 3]
                    cn["ob"] += 1
                    r0 = tt * TT + tb * 128
                    k.op("dve", lambda: nc.vector.tensor_tensor(
                        out=ot[:], in0=at[:], in1=x1[tb][0][:, n * 512:(n + 1) * 512], op=ALU.add),
                        reads=[ab, x1[tb][1]], writes=[obb])
                    k.dma("sp", obb.name + "_st", x_dst[r0:r0 + 128, n * 512:(n + 1) * 512], ot[:], reads=[obb])
        _barrier(k)


def build_fused(SL=S, depth=DEPTH, phases="123"):
    from contextlib import ExitStack
    nc = bass.Bass("TRN2", target_bir_lowering=False)
    ext = lambda name, shape, dt=F32: nc.dram_tensor(name, shape, dt, kind="ExternalInput").ap()
    x = ext("x", [SL, D])
    w_in = ext("w_in", [depth, D, D_IN])
    w_out = ext("w_out", [depth, D, D])
    w_up = ext("w_up", [depth, D, D_FF])
    w_down = ext("w_down", [depth, D_FF, D])
    g_attn = ext("g_attn", [depth, D])
    g_qk = ext("g_qk", [depth, 128])
    g_cat = ext("g_cat", [depth, D])
    g_mlp = ext("g_mlp", [depth, D])
    sink_bc = ext("sink_bc", [128, depth * 16])
    biasT_d = ext("biasT", [128, 2, 16, 128])
    maskT_d = ext("maskT", [128, 2, 128])
    masks_d = ext("masks", [128, 4, 512])
    ntri_d = ext("ntri", [128, 128])
    ident_d = ext("ident", [128, 128])
    xo = nc.dram_tensor("xo", [SL, D], F32, kind="ExternalOutput").ap()
    xa = nc.dram_tensor("x_scratch", [SL, D], F32).ap()
    o_s = nc.dram_tensor("o_scratch", [SL, D], F32).ap()
    scr = {
        "qaT": nc.dram_tensor("qaT_scr", [8, 128, SL], BF16).ap(),
        "kaT2": nc.dram_tensor("kaT2_scr", [4, 128, SL], BF16).ap(),
        "va": nc.dram_tensor("va_scr", [SL, 256], BF16).ap(),
        "qbT": nc.dram_tensor("qbT_scr", [8, 128, SL], BF16).ap(),
        "kbT": nc.dram_tensor("kbT_scr", [8, 128, SL], BF16).ap(),
        "vb": nc.dram_tensor("vb_scr", [SL, 1024], BF16).ap(),
    }
    wc = {nm: (nc.dram_tensor("wc_" + nm, shp, BF16).ap(), Buf("wc_" + nm))
          for nm, shp in (("in", [D, D_IN]), ("out", [D, D]), ("up", [D, D_FF]), ("down", [D_FF, D]))}
    with ExitStack() as st0:
        k = K(nc, st0)
        ident32 = (st0.enter_context(nc.sbuf_tensor("ident32", [128, 128], F32)), Buf("ident32"))
        ident = (st0.enter_context(nc.sbuf_tensor("identb", [128, 128], BF16)), Buf("identb"))
        with nc.Block():
            k.dma("sp", "ident32", ident32[0][:], ident_d, writes=[ident32[1]])
            k.op("dve", lambda: nc.vector.tensor_copy(out=ident[0][:], in_=ident32[0][:]),
                 reads=[ident32[1]], writes=[ident[1]])
            for l in range(depth):
                x_src = x if l == 0 else xa
                x_dst = xo if l == depth - 1 else xa
                for nm, src in (("in", w_in[l]), ("out", w_out[l]), ("up", w_up[l]), ("down", w_down[l])):
                    dst, cbuf = wc[nm]
                    for c0 in range(0, src.shape[1], 512):
                        k.dma("pool", cbuf.name, dst[:, c0:c0 + 512].rearrange("(k p) n -> p k n", p=128),
                              src[:, c0:c0 + 512].rearrange("(k p) n -> p k n", p=128), writes=[cbuf])
                if "1" in phases:
                    emit_P1(k, nc, "L%dp1_" % l, SL, x_src, wc["in"], g_attn[l:l + 1, :], g_qk[l:l + 1, :], ident, scr)
                for gi in (range(4) if "2" in phases else []):
                    emit_P2(k, nc, "L%dp2g%d" % (l, gi), SL, gi, scr, biasT_d, maskT_d,
                            sink_bc, l * 16 + 4 * gi, masks_d, ntri_d, o_s,
                            parts="".join(ch for ch in phases if ch in "ab") or "ab")
                if "3" in phases:
                    emit_P3(k, nc, "L%dp3_" % l, SL, x_src, x_dst, o_s, wc["out"], wc["up"], wc["down"],
                            g_cat[l:l + 1, :], g_mlp[l:l + 1, :], ident)
            _barrier(k)
    return nc


def _t5_bucket(dist):
    max_exact = 16
    d = np.maximum(dist, 0)
    ratio = np.maximum(d, 1).astype(np.float32) / max_exact
    large = max_exact + (np.log(ratio) / math.log(128 / max_exact) * (32 - max_exact)).astype(np.int32)
    large = np.minimum(large, 31)
    return np.where(d < max_exact, d, large).astype(np.int32)


def l2_constants():
    s = np.arange(128)[:, None, None]
    j = np.arange(2)[None, :, None]
    q = np.arange(128)[None, None, :]
    dist = q + 128 - (j * 128 + s)
    valid = (dist >= 0) & (dist < 128)
    maskT = np.where(valid, 0.0, NEG).astype(np.float32)
    bucket = _t5_bucket(dist)
    sa = (np.arange(4)[None, :, None] * 128 + np.arange(128)[:, None, None])
    masks = (sa < np.arange(512)[None, None, :]).astype(np.float32)
    ntri = -(np.arange(128)[:, None] >= np.arange(128)[None, :]).astype(np.float32)
    return bucket, maskT, masks, ntri


_PROG = {}


def make_inputs(x_seq, norm_attn_g, w_in, q_norm_g, k_norm_g, sinks, rel_bias, swa_out_g, sb_out_g, w_out,
                norm_mlp_g, w_up, w_down, depth):
    f32 = np.float32
    c = lambda a: np.ascontiguousarray(np.asarray(a, f32))
    bucket, maskT, masks, ntri = l2_constants()
    biasT = c(np.asarray(rel_bias, f32)[bucket].transpose(0, 1, 3, 2))
    sink_bc = c(np.broadcast_to(np.asarray(sinks, f32)[:depth].reshape(1, depth * 16), (128, depth * 16)))
    shared = {
        "w_in": c(w_in[:depth]), "w_out": c(w_out[:depth]), "w_up": c(w_up[:depth]), "w_down": c(w_down[:depth]),
        "g_attn": c(norm_attn_g[:depth]),
        "g_qk": c(np.concatenate([np.asarray(q_norm_g, f32)[:depth], np.asarray(k_norm_g, f32)[:depth]], 1)),
        "g_cat": c(np.concatenate([np.asarray(swa_out_g, f32)[:depth], np.asarray(sb_out_g, f32)[:depth]], 1)),
        "g_mlp": c(norm_mlp_g[:depth]),
        "sink_bc": sink_bc, "biasT": biasT, "maskT": maskT, "masks": masks, "ntri": ntri,
        "ident": np.eye(128, dtype=f32)}
    return [dict(shared, x=c(xs)) for xs in x_seq]


def kernel(x, norm_attn_g, w_in, q_norm_g, k_norm_g, sinks, rel_bias, swa_out_g, sb_out_g, w_out,
           norm_mlp_g, w_up, w_down):
    x = np.asarray(x, np.float32)
    nb, sl = x.shape[0], x.shape[1]
    key = (sl, DEPTH)
    if key not in _PROG:
        _PROG[key] = build_fused(sl, DEPTH)
    in_maps = make_inputs([x[b] for b in range(nb)], norm_attn_g, w_in, q_norm_g, k_norm_g, sinks, rel_bias,
                          swa_out_g, sb_out_g, w_out, norm_mlp_g, w_up, w_down, DEPTH)
    res = run_bass_kernel_spmd(_PROG[key], in_maps, core_ids=list(range(nb)))
    return np.stack([res.results[b]["xo"] for b in range(nb)], 0).astype(np.float32)
```
